# Optimizing a Trainium2 kernel written in Bass

```python
import math
import jax
import jax.numpy as jnp
from jax import lax
import numpy as np

D_MODEL = 1024
BATCH = 8
SEQ = 4096
DEPTH = 2

CHUNK = 64
Q_BLOCK = 128
RMS_EPS = 1e-6
SUBLN_EPS = 1e-5
D_MIX = D_MODEL
D_FF = 4 * D_MODEL
DA_WIDTH = D_MIX // 4
DA_V = 64
DA_HEADS = DA_WIDTH // DA_V
DA_QK = DA_V // 2
MLA_WIDTH = D_MIX // 4
MLA_V = 64
MLA_HEADS = MLA_WIDTH // MLA_V
MLA_NOPE = 64
MLA_ROPE = 32
MLA_Q_LORA = D_MODEL // 4
MLA_KV_LORA = D_MODEL // 8
ROPE_THETA = 10000.0
SSM_WIDTH = D_MIX - DA_WIDTH - MLA_WIDTH
SSM_HEAD_DIM = 64
SSM_HEADS = SSM_WIDTH // SSM_HEAD_DIM
SSM_GROUPS = 2
SSM_STATE = 64
CONV_WIDTH = 4
CONV_CH = SSM_WIDTH + 2 * SSM_GROUPS * SSM_STATE
IN_SIZES = (DA_WIDTH, DA_WIDTH, DA_WIDTH, MLA_Q_LORA, MLA_KV_LORA, MLA_ROPE, SSM_WIDTH, CONV_CH, SSM_HEADS)
D_IN = sum(IN_SIZES)

kernel_name = 'hybrid_chunk_causal_parallel_heads'


def _rmsnorm(t, g, eps=RMS_EPS):
    tf = t.astype(jnp.float32)
    tf = tf * lax.rsqrt(jnp.mean(tf * tf, axis=-1, keepdims=True) + eps)
    return (tf * g.astype(jnp.float32)).astype(t.dtype)


def _rope_tables(positions):
    inv_freq = 1.0 / (ROPE_THETA ** (jnp.arange(0, MLA_ROPE, 2, dtype=jnp.float32) / MLA_ROPE))
    ang = positions.astype(jnp.float32)[..., None] * inv_freq
    return jnp.cos(ang), jnp.sin(ang)


def _apply_rope(t, cos, sin):
    tf = t.astype(jnp.float32)
    t1, t2 = jnp.split(tf, 2, axis=-1)
    return jnp.concatenate([t1 * cos - t2 * sin, t1 * sin + t2 * cos], axis=-1).astype(t.dtype)


def _to_blocks(t):
    b, s = t.shape[0], t.shape[1]
    return jnp.moveaxis(t.reshape((b, s // Q_BLOCK, Q_BLOCK) + t.shape[2:]), 1, 0)


def _from_blocks(o):
    nb, b, qb = o.shape[0], o.shape[1], o.shape[2]
    return jnp.moveaxis(o, 0, 1).reshape((b, nb * qb) + o.shape[3:])


def _chunk_mask(q_start, seq):
    q_chunk = (q_start + jnp.arange(Q_BLOCK)) // CHUNK
    k_chunk = jnp.arange(seq) // CHUNK
    return k_chunk[None, :] <= q_chunk[:, None]


def _masked_softmax(s, mask):
    return jax.nn.softmax(jnp.where(mask, s, -jnp.inf), axis=-1)


def _diff_attention(q, k, v, lam, subln_g, lambda_init):
    b, s, _ = q.shape
    q = q.reshape(b, s, DA_HEADS, 2, DA_QK)
    k = k.reshape(b, s, DA_HEADS, 2, DA_QK)
    v = v.reshape(b, s, DA_HEADS, DA_V)
    k1, k2 = k[..., 0, :], k[..., 1, :]
    scale = DA_QK ** -0.5

    def block(args):
        start, q1b, q2b = args
        mask = _chunk_mask(start, s)
        s1 = jnp.einsum('bqhd,bkhd->bhqk', q1b, k1).astype(jnp.float32) * scale
        s2 = jnp.einsum('bqhd,bkhd->bhqk', q2b, k2).astype(jnp.float32) * scale
        p = _masked_softmax(s1, mask) - lam * _masked_softmax(s2, mask)
        return jnp.einsum('bhqk,bkhe->bqhe', p.astype(v.dtype), v)

    starts = jnp.arange(s // Q_BLOCK) * Q_BLOCK
    o = _from_blocks(lax.map(block, (starts, _to_blocks(q[..., 0, :]), _to_blocks(q[..., 1, :]))))
    o = _rmsnorm(o, subln_g, SUBLN_EPS) * (1.0 - lambda_init)
    return o.reshape(b, s, DA_WIDTH)


def _mla(cq, ckv, kr, cos, sin, q_norm_g, w_uq, kv_norm_g, w_ukv):
    b, s, _ = cq.shape
    q = (_rmsnorm(cq, q_norm_g) @ w_uq).reshape(b, s, MLA_HEADS, MLA_NOPE + MLA_ROPE)
    qn = q[..., :MLA_NOPE]
    qr = _apply_rope(q[..., MLA_NOPE:], cos[:, :, None, :], sin[:, :, None, :])
    kv = (_rmsnorm(ckv, kv_norm_g) @ w_ukv).reshape(b, s, MLA_HEADS, MLA_NOPE + MLA_V)
    kn, v = kv[..., :MLA_NOPE], kv[..., MLA_NOPE:]
    kr = _apply_rope(kr, cos, sin)
    scale = (MLA_NOPE + MLA_ROPE) ** -0.5

    def block(args):
        start, qnb, qrb = args
        sc = (jnp.einsum('bqhd,bkhd->bhqk', qnb, kn)
              + jnp.einsum('bqhr,bkr->bhqk', qrb, kr)).astype(jnp.float32) * scale
        p = _masked_softmax(sc, _chunk_mask(start, s))
        return jnp.einsum('bhqk,bkhe->bqhe', p.astype(v.dtype), v)

    starts = jnp.arange(s // Q_BLOCK) * Q_BLOCK
    o = _from_blocks(lax.map(block, (starts, _to_blocks(qn), _to_blocks(qr))))
    return o.reshape(b, s, MLA_WIDTH)


def _ssd_scan(xs, dt, a, bm, cm):
    b, l, h, p = xs.shape
    n = bm.shape[-1]
    nc = l // CHUNK
    xd = (xs * dt[..., None]).reshape(b, nc, CHUNK, h, p)
    bc = bm.reshape(b, nc, CHUNK, h, n)
    cc = cm.reshape(b, nc, CHUNK, h, n)
    a_cs = jnp.cumsum(jnp.moveaxis((dt * a).reshape(b, nc, CHUNK, h), 3, 1), axis=-1)
    seg = a_cs[..., :, None] - a_cs[..., None, :]
    causal = jnp.tril(jnp.ones((CHUNK, CHUNK), dtype=bool))
    decay_in = jnp.exp(jnp.where(causal, seg, -jnp.inf))
    scores = jnp.einsum('bclhn,bcshn->bhcls', cc, bc) * decay_in
    y_diag = jnp.einsum('bhcls,bcshp->bclhp', scores, xd)
    decay_to_end = jnp.exp(a_cs[..., -1:] - a_cs)
    chunk_states = jnp.einsum('bclhn,bhcl,bclhp->bchpn', bc, decay_to_end, xd)
    chunk_decay = jnp.exp(a_cs[..., -1])

    def step(state, inp):
        st, dec = inp
        return state * dec[..., None, None] + st, state

    init = jnp.zeros((b, h, p, n), jnp.float32)
    _, prev = lax.scan(step, init, (jnp.moveaxis(chunk_states, 1, 0), jnp.moveaxis(chunk_decay, 2, 0)))
    prev = jnp.moveaxis(prev, 0, 1)
    y_off = jnp.einsum('bclhn,bchpn,bhcl->bclhp', cc, prev, jnp.exp(a_cs))
    return (y_diag + y_off).reshape(b, l, h, p)


def _mamba2(z, xbc, dt_raw, conv_w, conv_b, dt_bias, a_log, d_skip, norm_g):
    b, s, _ = z.shape
    xbc = lax.conv_general_dilated(xbc, conv_w[:, None, :].astype(xbc.dtype), (1,), [(CONV_WIDTH - 1, 0)],
                                   dimension_numbers=('NWC', 'WIO', 'NWC'), feature_group_count=CONV_CH)
    xbc = jax.nn.silu(xbc + conv_b)
    xs, bm, cm = jnp.split(xbc.astype(jnp.float32), [SSM_WIDTH, SSM_WIDTH + SSM_GROUPS * SSM_STATE], axis=-1)
    xs = xs.reshape(b, s, SSM_HEADS, SSM_HEAD_DIM)
    rep = SSM_HEADS // SSM_GROUPS
    bm = jnp.repeat(bm.reshape(b, s, SSM_GROUPS, SSM_STATE), rep, axis=2)
    cm = jnp.repeat(cm.reshape(b, s, SSM_GROUPS, SSM_STATE), rep, axis=2)
    dt = jax.nn.softplus(dt_raw.astype(jnp.float32) + dt_bias.astype(jnp.float32))
    a = -jnp.exp(a_log.astype(jnp.float32))
    y = _ssd_scan(xs, dt, a, bm, cm) + xs * d_skip.astype(jnp.float32)[:, None]
    y = y.reshape(b, s, SSM_WIDTH) * jax.nn.silu(z.astype(jnp.float32))
    y = _rmsnorm(y.reshape(b, s, SSM_GROUPS, SSM_WIDTH // SSM_GROUPS), norm_g.reshape(SSM_GROUPS, -1))
    return y.reshape(b, s, SSM_WIDTH).astype(z.dtype)


def _hybrid_mixer(h, cos, sin, layer, w_in, lam_p, subln_g, q_norm_g, w_uq, kv_norm_g, w_ukv,
                  conv_w, conv_b, dt_bias, a_log, d_skip, ssm_norm_g, w_out):
    proj = h @ w_in
    splits = np.cumsum(IN_SIZES)[:-1].tolist()
    a_q, a_k, a_v, b_cq, b_ckv, b_kr, c_z, c_xbc, c_dt = jnp.split(proj, splits, axis=-1)
    lambda_init = 0.8 - 0.6 * math.exp(-0.3 * layer)
    lp = lam_p.astype(jnp.float32)
    lam = jnp.exp(jnp.sum(lp[0] * lp[1])) - jnp.exp(jnp.sum(lp[2] * lp[3])) + lambda_init
    y_a = _diff_attention(a_q, a_k, a_v, lam, subln_g, lambda_init)
    y_b = _mla(b_cq, b_ckv, b_kr, cos, sin, q_norm_g, w_uq, kv_norm_g, w_ukv)
    y_c = _mamba2(c_z, c_xbc, c_dt, conv_w, conv_b, dt_bias, a_log, d_skip, ssm_norm_g)
    return jnp.concatenate([y_a, y_b, y_c], axis=-1) @ w_out


def _sq_relu_mlp(h, w_up, w_down):
    return jnp.square(jax.nn.relu(h @ w_up)) @ w_down


def setup_inputs(seed: int = 0) -> dict:
    key = jax.random.key(seed)
    ks = jax.random.split(key, 24)

    def nrm(k, shape, scale):
        return jax.random.normal(k, shape, jnp.float32) * scale

    x = nrm(ks[0], (BATCH, SEQ, D_MODEL), 1.0)
    c = nrm(ks[1], (BATCH, D_MODEL), 1.0)
    offset = jax.random.randint(ks[2], (BATCH, 1), 0, 4096, dtype=jnp.int32)
    positions = offset + jnp.arange(SEQ, dtype=jnp.int32)[None, :]
    w_ada = nrm(ks[3], (DEPTH, D_MODEL, 6 * D_MODEL), 0.5 * D_MODEL ** -0.5)
    b_ada = nrm(ks[4], (DEPTH, 6 * D_MODEL), 0.02)
    norm_g = 1.0 + nrm(ks[5], (DEPTH, 4, D_MODEL), 0.02)
    w_in = nrm(ks[6], (DEPTH, D_MODEL, D_IN), D_MODEL ** -0.5)
    diff_lambda = nrm(ks[7], (DEPTH, 4, DA_QK), 0.1)
    diff_subln_g = 1.0 + nrm(ks[8], (DEPTH, DA_V), 0.02)
    mla_q_norm_g = 1.0 + nrm(ks[9], (DEPTH, MLA_Q_LORA), 0.02)
    w_uq = nrm(ks[10], (DEPTH, MLA_Q_LORA, MLA_HEADS * (MLA_NOPE + MLA_ROPE)), MLA_Q_LORA ** -0.5)
    mla_kv_norm_g = 1.0 + nrm(ks[11], (DEPTH, MLA_KV_LORA), 0.02)
    w_ukv = nrm(ks[12], (DEPTH, MLA_KV_LORA, MLA_HEADS * (MLA_NOPE + MLA_V)), MLA_KV_LORA ** -0.5)
    conv_w = nrm(ks[13], (DEPTH, CONV_WIDTH, CONV_CH), CONV_WIDTH ** -0.5)
    conv_b = nrm(ks[14], (DEPTH, CONV_CH), 0.02)
    dt0 = jnp.exp(jax.random.uniform(ks[15], (DEPTH, SSM_HEADS), jnp.float32,
                                     math.log(1e-3), math.log(1e-1)))
    dt_bias = dt0 + jnp.log(-jnp.expm1(-dt0))
    a_log = jnp.log(jax.random.uniform(ks[16], (DEPTH, SSM_HEADS), jnp.float32, 1.0, 16.0))
    d_skip = 1.0 + nrm(ks[17], (DEPTH, SSM_HEADS), 0.1)
    ssm_norm_g = 1.0 + nrm(ks[18], (DEPTH, SSM_WIDTH), 0.02)
    w_out = nrm(ks[19], (DEPTH, D_MIX, D_MODEL), D_MIX ** -0.5)
    w_up = nrm(ks[20], (DEPTH, D_MODEL, D_FF), D_MODEL ** -0.5)
    w_down = nrm(ks[21], (DEPTH, D_FF, D_MODEL), D_FF ** -0.5)
    return {'x': x, 'c': c, 'positions': positions, 'w_ada': w_ada, 'b_ada': b_ada, 'norm_g': norm_g,
            'w_in': w_in, 'diff_lambda': diff_lambda, 'diff_subln_g': diff_subln_g,
            'mla_q_norm_g': mla_q_norm_g, 'w_uq': w_uq, 'mla_kv_norm_g': mla_kv_norm_g, 'w_ukv': w_ukv,
            'conv_w': conv_w, 'conv_b': conv_b, 'dt_bias': dt_bias, 'a_log': a_log, 'd_skip': d_skip,
            'ssm_norm_g': ssm_norm_g, 'w_out': w_out, 'w_up': w_up, 'w_down': w_down}


def reference(x, c, positions, w_ada, b_ada, norm_g, w_in, diff_lambda, diff_subln_g,
              mla_q_norm_g, w_uq, mla_kv_norm_g, w_ukv, conv_w, conv_b, dt_bias, a_log, d_skip,
              ssm_norm_g, w_out, w_up, w_down):
    cos, sin = _rope_tables(positions)
    cond = jax.nn.silu(c)
    for l in range(DEPTH):
        mod = (cond @ w_ada[l] + b_ada[l])[:, None, :]
        sh_m, sc_m, g_m, sh_f, sc_f, g_f = jnp.split(mod, 6, axis=-1)
        h = _rmsnorm(x, norm_g[l, 0]) * (1.0 + sc_m) + sh_m
        y = _hybrid_mixer(h, cos, sin, l, w_in[l], diff_lambda[l], diff_subln_g[l],
                          mla_q_norm_g[l], w_uq[l], mla_kv_norm_g[l], w_ukv[l],
                          conv_w[l], conv_b[l], dt_bias[l], a_log[l], d_skip[l], ssm_norm_g[l], w_out[l])
        x = x + g_m * _rmsnorm(y, norm_g[l, 1])
        h = _rmsnorm(x, norm_g[l, 2]) * (1.0 + sc_f) + sh_f
        y = _sq_relu_mlp(h, w_up[l], w_down[l])
        x = x + g_f * _rmsnorm(y, norm_g[l, 3])
    return x
```

```python
import math
from contextlib import ExitStack

import numpy as np
import ml_dtypes

import concourse.bass as bass
import concourse.mybir as mybir
from concourse.ap import AP
from concourse.bass_utils import run_bass_kernel_spmd

F32 = mybir.dt.float32
BF16 = mybir.dt.bfloat16
I32 = mybir.dt.int32
AF = mybir.ActivationFunctionType
ALU = mybir.AluOpType

P = 128
D = 1024
KC = 8
T = 4096
NT = 512
NG = T // NT
DEPTH = 2
NPIECE = 24
NSP = 114
NRP = 664
RMS_EPS = 1e-6
SUBLN_EPS = 1e-5
TWO_PI = 2.0 * math.pi
CW1 = 6.28125
CW2 = TWO_PI - CW1
MAGIC = 12582912.0


def _box(a):
    if isinstance(a, tuple):
        return (a, 0, 1, 0, 1)
    shp = a.tensor.shape
    rowlen = 1
    for s in shp[1:]:
        rowlen *= s
    es = mybir.dt.size(a.dtype)
    off = a.offset
    aps = a.ap
    p0 = off // rowlen
    f0 = off % rowlen
    ext = 1
    for st, cn in aps[1:]:
        ext += (cn - 1) * abs(st)
    lo, hi = f0 * es, (f0 + ext) * es
    if a.tensor.name == "PS":
        return ("PS", 0, P, (lo // 2048) * 2048, ((hi + 2047) // 2048) * 2048)
    return (a.tensor.name, p0, p0 + aps[0][1], lo, hi)


class Sched:
    MAXV = 12000

    def __init__(self, nc, es):
        self.nc, self.es = nc, es
        self.eng = {'pe': nc.tensor, 'act': nc.scalar, 'dve': nc.vector, 'pool': nc.gpsimd,
                    'sp': nc.sync}
        self.nsem = 0
        self.csem, self.ccnt = {}, {}
        for e in ('pe', 'act', 'dve', 'pool'):
            self._rot(e)
        self.seen = {e: {} for e in self.eng}
        self.recs = {}
        self.dsem, self.dnext = {}, {}
        for q, k in (('sp', 24), ('pool', 8), ('act', 6)):
            self.dsem[q] = [[self._new("d%s%d" % (q, i)), 0] for i in range(k)]
            self.dnext[q] = 0
        self.nops = 0

    def _new(self, name):
        self.nsem += 1
        return self.es.enter_context(self.nc.semaphore(name))

    def _rot(self, e):
        self.csem[e] = self._new("c%s%d" % (e, self.nsem))
        self.ccnt[e] = 0

    def _deps(self, rb, wb, e=None):
        out = []
        for b in rb:
            ps = (b[0] == "PS")
            for r in self.recs.get(b[0], ()):
                if (r[5] or (ps and r[4][2] != e)) and r[0] < b[2] and b[1] < r[1] and r[2] < b[4] and b[3] < r[3]:
                    out.append(r[4])
        for b in wb:
            for r in self.recs.get(b[0], ()):
                if r[0] < b[2] and b[1] < r[1] and r[2] < b[4] and b[3] < r[3]:
                    out.append(r[4])
        return out

    def _rec(self, b, tok, isw):
        lst = self.recs.setdefault(b[0], [])
        p0, p1, lo, hi = b[1], b[2], b[3], b[4]
        if isw:
            lst[:] = [r for r in lst
                      if not (p0 <= r[0] and r[1] <= p1 and lo <= r[2] and r[3] <= hi)]
        else:
            te = tok[2]
            if te != 'dma':
                lst[:] = [r for r in lst
                          if not ((not r[5]) and r[4][2] == te and p0 <= r[0] and r[1] <= p1
                                  and lo <= r[2] and r[3] <= hi)]
        lst.append((p0, p1, lo, hi, tok, isw))

    def _wait(self, e, tok):
        sem, val, te = tok
        if te == e and e == 'pe':
            return
        if self.seen[e].get(sem.num, 0) >= val:
            return
        if te == 'pe' and sem is self.csem['pe'] and val > self.ccnt['pe']:
            raise RuntimeError("wait on a future PE milestone (mark the matmul ms=True)")
        self.eng[e].wait_ge(sem, val)
        self.seen[e][sem.num] = val

    def op(self, e, fn, reads=(), writes=(), ms=True):
        rb = [_box(a) for a in reads]
        wb = [_box(a) for a in writes]
        for t in self._deps(rb, wb, e):
            self._wait(e, t)
        ins = fn()
        self.nops += 1
        if e == 'pe' and not ms:
            tok = (self.csem[e], self.ccnt[e] + 1, e)
        else:
            self.ccnt[e] += 1
            ins.then_inc(self.csem[e], 1)
            tok = (self.csem[e], self.ccnt[e], e)
        for b in rb:
            self._rec(b, tok, False)
        for b in wb:
            self._rec(b, tok, True)
        if ms and self.ccnt[e] >= self.MAXV:
            self._rot(e)
        return tok

    def dma(self, q, out, in_, reads=(), writes=()):
        rb = [_box(a) for a in reads]
        wb = [_box(a) for a in writes]
        for t in self._deps(rb, wb, q):
            self._wait(q, t)
        slot = self.dsem[q][self.dnext[q]]
        self.dnext[q] = (self.dnext[q] + 1) % len(self.dsem[q])
        if slot[1]:
            self._wait(q, (slot[0], 16 * slot[1], 'dma'))
        ins = self.eng[q].dma_start(out=out, in_=in_)
        ins.then_inc(slot[0], 16)
        slot[1] += 1
        tok = (slot[0], 16 * slot[1], 'dma')
        for b in rb:
            self._rec(b, tok, False)
        for b in wb:
            self._rec(b, tok, True)
        return tok

    def wait_tok(self, e, tok):
        self._wait(e, tok)


def _fr(a, dims):
    return AP(a.tensor, a.offset, [list(a.ap[0])] + [list(d) for d in dims])


def _colpiece(M):
    return np.ascontiguousarray(M.reshape(KC, P, 512).transpose(1, 0, 2).reshape(P, 4096))


def _prep_pieces(w_in, w_out, w_up, w_down):
    pcs = np.zeros((NPIECE, P, 4096), np.float32)
    z = lambda n: np.zeros((D, n), np.float32)
    kr = w_in[:, 1152:1184]
    krs = np.concatenate([kr[:, 16:32], kr[:, 0:16]], 1)
    mats = [
        np.concatenate([w_in[:, 0:256], w_in[:, 256:512]], 1),
        np.concatenate([w_in[:, 768:1024], w_in[:, 1024:1152], kr, kr, kr, kr], 1),
        np.concatenate([krs, krs, krs, krs, w_in[:, 512:768], w_in[:, 2464:2472], z(120)], 1),
        w_in[:, 1696:2208],
        np.concatenate([w_in[:, 2208:2336], w_in[:, 2336:2464], z(256)], 1),
        w_in[:, 1184:1696],
    ]
    for i, M in enumerate(mats):
        pcs[i] = _colpiece(np.ascontiguousarray(M))
    for j in range(2):
        pcs[6 + j] = _colpiece(np.ascontiguousarray(w_out[:, 512 * j:512 * (j + 1)]))
    for j in range(8):
        pcs[8 + j] = _colpiece(np.ascontiguousarray(w_up[:, 512 * j:512 * (j + 1)]))
    for m in range(8):
        pcs[16 + m] = np.ascontiguousarray(
            w_down[:, 128 * m:128 * (m + 1)].reshape(32, P, 128).transpose(1, 0, 2)).reshape(P, 4096)
    return pcs


def _prep_shared(inp):
    f = lambda k: np.asarray(inp[k], np.float32)
    w_in, w_out, w_up, w_down = f('w_in'), f('w_out'), f('w_up'), f('w_down')
    wp = np.zeros((DEPTH, NPIECE, P, 4096), np.float32)
    for l in range(DEPTH):
        wp[l] = _prep_pieces(w_in[l], w_out[l], w_up[l], w_down[l])
    wp = wp.reshape(DEPTH * NPIECE * P, 4096)
    norm_g, b_ada = f('norm_g'), f('b_ada')
    conv_w, conv_b = f('conv_w'), f('conv_b')
    sp = np.zeros((DEPTH, P, NSP), np.float32)
    rp = np.zeros((DEPTH, NRP), np.float32)
    wuq = np.zeros((DEPTH, P, 2 * 512), np.float32)
    wukv = np.zeros((DEPTH, P, 512), np.float32)
    for l in range(DEPTH):
        for i in range(4):
            sp[l, :, 8 * i:8 * i + 8] = norm_g[l, i].reshape(8, P).T
        sp[l, :, 32:80] = b_ada[l].reshape(48, P).T
        sp[l, :, 80:104] = conv_w[l].reshape(4, 6, P).transpose(2, 1, 0).reshape(P, 24)
        sp[l, :, 104:110] = conv_b[l].reshape(6, P).T
        sp[l, :, 110] = np.tile(f('diff_subln_g')[l], 2)
        sp[l, :, 111:113] = f('mla_q_norm_g')[l].reshape(2, P).T
        sp[l, :, 113] = f('mla_kv_norm_g')[l]
        rp[l, 0:8] = f('dt_bias')[l]
        rp[l, 8:16] = f('a_log')[l]
        rp[l, 16:24] = f('d_skip')[l]
        rp[l, 24:536] = f('ssm_norm_g')[l]
        rp[l, 536:664] = f('diff_lambda')[l].reshape(-1)
        wq = f('w_uq')[l].reshape(256, 4, 96)
        nope, rope = wq[:, :, 0:64], wq[:, :, 64:96]
        ropes = np.concatenate([rope[:, :, 16:32], rope[:, :, 0:16]], 2)
        tiles = [nope[:, :, 0:32].reshape(256, 128), nope[:, :, 32:64].reshape(256, 128),
                 rope.reshape(256, 128), ropes.reshape(256, 128)]
        wqP = np.concatenate(tiles, 1)
        wuq[l] = wqP.reshape(2, P, 512).transpose(1, 0, 2).reshape(P, 1024)
        wk = f('w_ukv')[l].reshape(128, 4, 128)
        kn, v = wk[:, :, 0:64], wk[:, :, 64:128]
        wukv[l] = np.concatenate([kn[:, :, 0:32].reshape(128, 128), kn[:, :, 32:64].reshape(128, 128),
                                  v.reshape(128, 256)], 1)
    cbf = np.zeros((P, 384), np.float32)
    cbf[:, 0:128] = np.eye(P)
    cbf[:, 128:256] = 1.0
    cbf[0:64, 256:320] = 1.0
    cbf[64:128, 320:384] = 1.0
    cbf = cbf.astype(ml_dtypes.bfloat16)
    cf = np.zeros((P, 392), np.float32)
    j = np.arange(P)
    cf[:, 0:128] = (j[:, None] > j[None, :])
    cf[:, 128:256] = (j[:, None] <= j[None, :])
    cf[:, 256:384] = 1.0
    invf = (np.float32(1.0) / np.power(np.float32(10000.0),
                                       np.arange(0, 32, 2, dtype=np.float32) / np.float32(32))).astype(np.float32)
    cf[:, 384] = np.tile(invf, 8)
    cf[:, 385] = 0.25
    cf[:, 386] = np.tile(np.concatenate([np.full(16, 0.5), np.zeros(16)]), 4)
    cf[:, 387] = math.pi / 2
    cf[:, 388] = np.tile(np.concatenate([np.full(16, math.pi), np.zeros(16)]), 4)
    cf[:, 389] = RMS_EPS
    cf[:, 390] = SUBLN_EPS
    cf[:, 391] = 1.0
    return dict(wada=np.ascontiguousarray(f('w_ada')), wpieces=wp, sp=sp, rp=rp, wuq=wuq, wukv=wukv,
                cbf=cbf, cf=cf)


class _Stop(Exception):
    pass


def build(ngr=NG, depth=DEPTH, dbg=False, stop=99):
    nc = bass.Bass("TRN2", target_bir_lowering=False)

    def dram(name, shape, dt, kind):
        return nc.dram_tensor(name, shape, dt, kind=kind).ap()

    xT_d = dram("xT", [D, T], F32, "ExternalInput")
    pos_d = dram("pos", [1, T], I32, "ExternalInput")
    cv_d = dram("cvec", [P, KC], F32, "ExternalInput")
    wada_d = dram("wada", [DEPTH, D, 6 * D], F32, "ExternalInput")
    wp_d = dram("wpieces", [DEPTH * NPIECE * P, 4096], F32, "ExternalInput")
    sp_d = dram("sp", [DEPTH, P, NSP], F32, "ExternalInput")
    rp_d = dram("rp", [DEPTH, NRP], F32, "ExternalInput")
    wuq_d = dram("wuq", [DEPTH, P, 1024], F32, "ExternalInput")
    wukv_d = dram("wukv", [DEPTH, P, 512], F32, "ExternalInput")
    cbf_d = dram("cbf", [P, 384], BF16, "ExternalInput")
    cf_d = dram("cf", [P, 392], F32, "ExternalInput")
    out_d = dram("outT", [D, T], F32, "ExternalOutput")
    wbf_d = dram("wbf", [DEPTH * NPIECE * P, 4096], BF16, "Internal")
    rope_d = dram("ropetab", [2, 32, T], F32, "Internal")
    xmid_d = dram("xmid", [D, T], F32, "Internal")
    final_toks = []

    es = ExitStack()
    with es:
        S = Sched(nc, es)

        def sb(name, shape, dt):
            return es.enter_context(nc.sbuf_tensor(name, shape, dt))

        NWR = 3
        PS = es.enter_context(nc.psum_tensor("PS", [P, 8, 512], F32))
        CBF = sb("CBF", [P, 384], BF16)
        CF = sb("CF", [P, 392], F32)
        SPT = sb("SPT", [P, DEPTH, NSP], F32)
        RPT = sb("RPT", [P, NRP], F32)
        MODT = sb("MODT", [P, DEPTH, 48], F32)
        DER = sb("DER", [P, DEPTH, 32], F32)
        SM = sb("SM", [P, 64], F32)
        COND = sb("COND", [P, KC], F32)
        WUQ = sb("WUQ", [P, 1024], BF16)
        WUKV = sb("WUKV", [P, 512], BF16)
        XGs = [sb("XG0", [P, KC, NT], F32), sb("XG1", [P, KC, NT], F32)]
        HY = sb("HY", [P, KC, NT], BF16)
        SQ = sb("SQ", [P, KC, NT], BF16)
        RS = sb("RS", [P, 2, NT], F32)
        SCR = sb("SCR", [P, 8, NT], F32)
        WR = sb("WR", [P, NWR, 4096], BF16)
        KTA = sb("KTA", [P, 2, T], BF16)
        VPA = sb("VPA", [P, 32 * 4 * 65 + 64], BF16)
        KNB = sb("KNB", [P, 2, T], BF16)
        KRB = sb("KRB", [P, T], BF16)
        VPB = sb("VPB", [P, 32 * 4 * 65 + 64], BF16)
        QTA = sb("QTA", [P, 2, NT], BF16)
        QNB = sb("QNB", [P, 2, NT], BF16)
        QRB = sb("QRB", [P, NT], BF16)
        CKVN = sb("CKVN", [P, NT], BF16)
        PT = sb("PT", [P, 2, 4, NT], BF16)
        ROPE = sb("ROPE", [P, 2, NT], F32)
        XB = sb("XB", [P, 2, 516], F32)
        HALO = sb("HALO", [P, 6, 4], F32)
        BCT = sb("BCT", [P, 2, NT], BF16)
        CQN = BCT
        ZS = sb("ZS", [P, 4, NT], BF16)
        BTOK = sb("BTOK", [P, 4, 192], BF16)
        DTS = sb("DTS", [P, 8, 32], F32)
        ST32 = sb("ST32", [P, 256], F32)
        STB = sb("STB", [P, 256], BF16)
        YN = sb("YN", [P, NT], BF16)

        ident = CBF[:, 0:128]
        ones_bf = CBF[:, 128:256]
        blk_bf = CBF[:, 256:384]
        Umat = CF[:, 0:128]
        Lm = CF[:, 128:256]
        ones_f = CF[:, 256:384]
        epsR = CF[:, 389:390]
        epsS = CF[:, 390:391]
        onec = CF[:, 391:392]

        XST = SQ[:, 0:4, :]
        XTOK = SQ[:, 4:8, :]
        PTf = PT[:].rearrange("p a b c -> p (a b c)")
        SCRb = SCR[:].bitcast(BF16)
        AT = ([SCRb[:, i // 2, (i % 2) * 512:(i % 2) * 512 + 512] for i in range(16)]
              + [PT[:, i // 4, i % 4, :] for i in range(8)] + [SQ[:, i, :] for i in range(8)])
        WRf = WR[:].bitcast(F32)
        ZSf = ZS[:].bitcast(F32)
        YM = [ROPE[:, 0, :], ROPE[:, 1, :],
              ZSf[:, 0:2, :].rearrange("p a b -> p (a b)"), ZSf[:, 2:4, :].rearrange("p a b -> p (a b)"),
              XB[:, 0, 0:512], XB[:, 1, 0:512],
              QTA[:].bitcast(F32).rearrange("p a b -> p (a b)"),
              QNB[:].bitcast(F32).rearrange("p a b -> p (a b)")]

        _bank = [0]

        def bank(lo=0, hi=8):
            b = _bank[0]
            if b < lo or b >= hi:
                b = lo
            _bank[0] = b + 1 if b + 1 < hi else lo
            return b

        def isap(x):
            return hasattr(x, 'tensor')

        def MM(out, lhsT, rhs, start=True, stop=True, ms=None, tp=None):
            kw = {'tile_position': tp} if tp is not None else {}
            S.op('pe', lambda: nc.tensor.matmul(out, lhsT=lhsT, rhs=rhs, start=start, stop=stop, **kw),
                 reads=[lhsT, rhs], writes=[out], ms=(stop if ms is None else ms))

        def TR(out, in_):
            S.op('pe', lambda: nc.tensor.transpose(out, in_, ident), reads=[in_, ident], writes=[out])

        def ACT(out, in_, func, scale=1.0, bias=None, accum=None):
            rd, wr, kw = [in_], [out], {}
            if isap(scale):
                rd.append(scale)
            if bias is not None:
                kw['bias'] = bias
                if isap(bias):
                    rd.append(bias)
            if accum is not None:
                kw['accum_out'] = accum
                wr.append(accum)
            S.op('act', lambda: nc.scalar.activation(out=out, in_=in_, func=func, scale=scale, **kw),
                 reads=rd, writes=wr)

        def TT(e, out, in0, in1, op):
            S.op(e, lambda: S.eng[e].tensor_tensor(out=out, in0=in0, in1=in1, op=op),
                 reads=[in0, in1], writes=[out])

        def TS(e, out, in0, s1, op0, s2=None, op1=None):
            rd = [in0] + [s for s in (s1, s2) if isap(s)]
            kw = {} if op1 is None else {'op1': op1}
            S.op(e, lambda: S.eng[e].tensor_scalar(out=out, in0=in0, scalar1=s1, scalar2=s2, op0=op0, **kw),
                 reads=rd, writes=[out])

        def STT(out, in0, sc, in1, op0, op1):
            rd = [in0, in1] + ([sc] if isap(sc) else [])
            S.op('dve', lambda: nc.vector.scalar_tensor_tensor(out=out, in0=in0, scalar=sc, in1=in1,
                                                               op0=op0, op1=op1), reads=rd, writes=[out])

        def CP(e, out, in_):
            if e == 'act':
                ACT(out, in_, AF.Copy)
            else:
                S.op(e, lambda: S.eng[e].tensor_copy(out=out, in_=in_), reads=[in_], writes=[out])

        def MS(e, out, val):
            S.op(e, lambda: S.eng[e].memset(out, val), writes=[out])

        def RCP(out, in_):
            S.op('dve', lambda: nc.vector.reciprocal(out=out, in_=in_), reads=[in_], writes=[out])

        def DMA(out, in_, rk=None, wk=None, q='sp'):
            rd = [in_] if rk is None else [rk]
            wr = [out] if wk is None else [wk]
            return S.dma(q, out, in_, reads=rd, writes=wr)

        _ev = [0]

        def evac():
            _ev[0] ^= 1
            return 'act' if _ev[0] else 'dve'

        def dump(name, ap, shape):
            if not dbg:
                return
            d = dram("dbg_" + name, list(shape), ap.dtype, "ExternalOutput")
            final_toks.append(DMA(d, ap, wk=('dbg', name)))

        def rstd_from(ps_ap, out_rs, nfeat, epscol):
            ACT(out_rs, ps_ap, AF.Ln, scale=1.0 / nfeat, bias=epscol)
            ACT(out_rs, out_rs, AF.Exp, scale=-0.5)

        DMA(CBF[:], cbf_d, rk=('in', 'cbf'))
        DMA(CF[:], cf_d, rk=('in', 'cf'))
        DMA(SPT[:], sp_d.rearrange("l p n -> p l n"), rk=('in', 'sp'))
        DMA(COND[:], cv_d, rk=('in', 'cv'))
        def vview(VP, kt0, nkt):
            return VP[:, kt0 * 260:(kt0 + nkt) * 260].rearrange("p (k h e) -> p k h e", h=4, e=65)
        for VP in (VPA, VPB):
            MS('pool', VP[:], 0.0)
            MS('pool', vview(VP, 0, 32)[:, :, :, 64:65], 1.0)
        MS('pool', BTOK[:], 0.0)

        _cast = [[1, 0]]

        def cast_upto(l, i):
            cur = _cast[0]
            while (cur[0], cur[1]) <= (l, min(i, NPIECE - 1)) and cur[0] < depth:
                r0 = (cur[0] * NPIECE + cur[1]) * P
                DMA(wbf_d[r0:r0 + P, :], wp_d[r0:r0 + P, :], rk=('in', 'wp'), wk=('wbf', cur[0], cur[1]), q='pool')
                cur[1] += 1
                if cur[1] == NPIECE:
                    cur[0] += 1
                    cur[1] = 0

        PRO = 9

        ACT(COND[:], COND[:], AF.Silu)
        RT = SCR[0:32, :, :]
        for g in range(ngr if PRO >= 4 else 0):
            posi = RT[:, 0, :].bitcast(I32)
            DMA(posi, AP(pos_d.tensor, g * NT, [[0, 32], [1, NT]]), rk=('in', 'pos'), q='act')
            CP('dve', RT[:, 1, :], posi)
            TS('dve', RT[:, 1, :], RT[:, 1, :], CF[0:32, 384:385], ALU.mult)
            for tb in range(2):
                TS('dve', RT[:, 2, :], RT[:, 1, :], 1.0 / TWO_PI, ALU.mult, CF[0:32, 385 + tb:386 + tb], ALU.add)
                TS('dve', RT[:, 2, :], RT[:, 2, :], MAGIC, ALU.add, MAGIC, ALU.subtract)
                STT(RT[:, 3, :], RT[:, 2, :], -CW1, RT[:, 1, :], ALU.mult, ALU.add)
                STT(RT[:, 3, :], RT[:, 2, :], -CW2, RT[:, 3, :], ALU.mult, ALU.add)
                TS('dve', RT[:, 3, :], RT[:, 3, :], CF[0:32, 387 + tb:388 + tb], ALU.add, math.pi, ALU.min)
                TS('dve', RT[:, 3, :], RT[:, 3, :], -math.pi, ALU.max)
                ACT(RT[:, 4 + tb, :], RT[:, 3, :], AF.Sin)
                DMA(rope_d[tb, :, g * NT:(g + 1) * NT], RT[:, 4 + tb, :], wk=('rope', tb, g), q='act')

        def mod_derive(l):
            STT(DER[:, l, 0:8], MODT[:, l, 8:16], 1.0, SPT[:, l, 0:8], ALU.add, ALU.mult)
            TT('dve', DER[:, l, 8:16], MODT[:, l, 16:24], SPT[:, l, 8:16], ALU.mult)
            STT(DER[:, l, 16:24], MODT[:, l, 32:40], 1.0, SPT[:, l, 16:24], ALU.add, ALU.mult)
            TT('dve', DER[:, l, 24:32], MODT[:, l, 40:48], SPT[:, l, 24:32], ALU.mult)

        _mod1 = [0]

        def mod_deferred(n):
            if depth < 2:
                return
            for _ in range(n):
                m = _mod1[0]
                if m >= 48:
                    return
                _mod1[0] += 1
                k = m % 3
                stg = SCR[:, 2 * k:2 * k + 2, :].rearrange("p a (c n) -> p (a c) n", n=128)
                DMA(stg, wada_v[1, :, :, m * 128:(m + 1) * 128], rk=('in', 'wada'))
                b = bank()
                for kc in range(KC):
                    MM(PS[:, b, 0:1], stg[:, kc, :], COND[:, kc:kc + 1], start=(kc == 0), stop=(kc == KC - 1))
                TT('dve', MODT[:, 1, m:m + 1], PS[:, b, 0:1], SPT[:, 1, 32 + m:33 + m], ALU.add)

        wada_v = wada_d.rearrange("l (kc p) n -> l p kc n", p=P)
        ri = 0
        for l in range(depth):
            for i in range(24):
                slot = ri % NWR
                ri += 1
                dst = WRf[:, slot, :].rearrange("p (kc n) -> p kc n", kc=KC)
                DMA(dst, wada_v[l, :, :, i * 256:(i + 1) * 256], rk=('in', 'wada'))
                for mt in range(2):
                    m = 2 * i + mt
                    for kc in range(KC):
                        MM(PS[:, 7, l * 48 + m:l * 48 + m + 1],
                           WRf[:, slot, kc * 256 + mt * 128:kc * 256 + mt * 128 + 128],
                           COND[:, kc:kc + 1], start=(kc == 0), stop=(kc == KC - 1))
                if l > 0:
                    continue
                stg = XGs[i % 2][:].rearrange("p a b -> p (a b)")
                outb = (PTf, SQ[:].rearrange("p a b -> p (a b)"), HY[:].rearrange("p a b -> p (a b)"))[i % 3]
                r0 = i * P
                DMA(stg, wp_d[r0:r0 + P, :], rk=('in', 'wp'))
                CP('act' if i % 2 else 'dve', outb, stg)
                DMA(wbf_d[r0:r0 + P, :], outb, wk=('wbf', 0, i), q='act')
            TT('dve', MODT[:, l, :], PS[:, 7, l * 48:(l + 1) * 48], SPT[:, l, 32:80], ALU.add)
            mod_derive(l)
        if PRO >= 3:
            dump("modt", MODT[:, 0:1, :], [P, 1, 48])

        order = [(l, g, i) for l in range(depth) for g in range(ngr) for i in range(NPIECE)]
        emitted = [0]
        ring0 = ri

        def need(seq):
            lim = min(seq + NWR - 1, len(order) - 1)
            while emitted[0] <= lim:
                n = emitted[0]
                l, g, i = order[n]
                if g == 0:
                    cast_upto(l, i + 6)
                r0 = (l * NPIECE + i) * P
                DMA(WR[:, (ring0 + n) % NWR, :], wbf_d[r0:r0 + P, :], rk=('wbf', l, i))
                emitted[0] += 1
            return (ring0 + seq) % NWR

        def seqof(l, g, i):
            return (l * ngr + g) * NPIECE + i

        def layer_setup(l):
            lam_init = 0.8 - 0.6 * math.exp(-0.3 * l)
            DMA(RPT[:], AP(rp_d.tensor, l * NRP, [[0, P], [1, NRP]]), rk=('in', 'rp'))
            ACT(SM[:, 0:8], RPT[:, 8:16], AF.Exp)
            TS('dve', SM[:, 0:8], SM[:, 0:8], -1.0, ALU.mult)
            TT('dve', SM[:, 16:48], RPT[:, 536:568], RPT[:, 568:600], ALU.mult)
            ACT(SM[:, 16:48], SM[:, 16:48], AF.Identity, accum=SM[:, 8:9])
            TT('dve', SM[:, 16:48], RPT[:, 600:632], RPT[:, 632:664], ALU.mult)
            ACT(SM[:, 16:48], SM[:, 16:48], AF.Identity, accum=SM[:, 9:10])
            ACT(SM[:, 8:10], SM[:, 8:10], AF.Exp)
            TT('dve', SM[:, 10:11], SM[:, 9:10], SM[:, 8:9], ALU.subtract)
            TS('dve', SM[:, 10:11], SM[:, 10:11], -lam_init, ALU.add)
            TS('dve', SM[:, 11:12], SPT[:, l, 110:111], 1.0 - lam_init, ALU.mult)
            stage = SCR[:, 0:3, :].rearrange("p a b -> p (a b)")
            DMA(stage[:, 0:1024], wuq_d[l], rk=('in', 'wuq'))
            DMA(stage[:, 1024:1536], wukv_d[l], rk=('in', 'wukv'))
            CP('dve', WUQ[:], stage[:, 0:1024])
            CP('dve', WUKV[:], stage[:, 1024:1536])
            MS('pool', HALO[:], 0.0)
            MS('pool', ST32[:], 0.0)
            MS('pool', STB[:], 0.0)

        def norm_to_h(XG, l, gcol, shcol, presq=False, prestat=False):
            rs = RS[:, 1, :] if prestat else RS[:, 0, :]
            if not prestat:
                if not presq:
                    ACT(SQ[:], XG[:], AF.Square)
                b = bank()
                for c in range(KC):
                    MM(PS[:, b, :], ones_bf, SQ[:, c, :], start=(c == 0), stop=(c == KC - 1))
                rstd_from(PS[:, b, :], rs, D, epsR)
            for c in range(KC):
                tmp = SCR[:, 6 + (c % 2), :]
                TT('dve', tmp, XG[:, c, :], rs, ALU.mult)
                ACT(HY[:, c, :], tmp, AF.Identity, scale=DER[:, l, gcol + c:gcol + c + 1],
                    bias=MODT[:, l, shcol + c:shcol + c + 1])

        def attn_core(g, qk_fn, vp, scale):
            nkt = 4 * g + 4

            def c0of(kt):
                d = kt - 4 * g
                return 128 * d if d > 0 else 0

            def qk_exp(kt):
                c0 = c0of(kt)
                qk_fn(kt, c0)
                buf = kt % 2
                ACT(PT[:, buf, :, c0:NT], PS[:, 0:4, c0:NT], AF.Exp, scale=scale)
                if kt - 4 * g >= 0:
                    MS('pool', PT[64:128, buf, :, c0:c0 + 64], 0.0)

            qk_exp(0)
            for kt in range(nkt):
                if kt + 1 < nkt:
                    qk_exp(kt + 1)
                c0 = c0of(kt)
                for r in range(4):
                    MM(PS[:, 4 + r, c0:NT], vp(kt, r), PT[:, kt % 2, r, c0:NT],
                       start=(kt == 0), stop=(kt == nkt - 1), ms=(r == 3))
            ACT(SCR[64:65, 4:8, :], PS[64:65, 4:8, :], AF.Ln)
            for r in range(4):
                MM(PS[0:64, r, :], ones_f[64:65, 0:64], SCR[64:65, 4 + r, :], ms=(r == 3))
            BCS = SCR[0:64, 4:8, :]
            ACT(BCS, PS[0:64, 0:4, :], AF.Exp, scale=-1.0)
            return BCS

        PTq = PT[:].rearrange("p a (b s) n -> p (a b) s n", s=2)

        def attn_pass(g, qk_fn, vp, scale):
            nkt = 4 * g + 4

            def c0of(kt):
                d = kt - 4 * g
                return 128 * d if d > 0 else 0

            def qk_exp(kt):
                c0 = c0of(kt)
                sb_ = 2 * (kt % 2)
                qk_fn(kt, c0, sb_)
                ACT(PTq[:, kt % 4, :, c0:NT], PS[:, sb_:sb_ + 2, c0:NT], AF.Exp, scale=scale)
                if kt - 4 * g >= 0:
                    MS('pool', PTq[64:128, kt % 4, :, c0:c0 + 64], 0.0)

            qk_exp(0)
            for kt in range(nkt):
                if kt + 1 < nkt:
                    qk_exp(kt + 1)
                c0 = c0of(kt)
                for i in range(2):
                    MM(PS[:, 4 + i, c0:NT], vp(kt, i), PTq[:, kt % 4, i, c0:NT],
                       start=(kt == 0), stop=(kt == nkt - 1), ms=(i == 1))
            ACT(SCR[64:65, 4:6, :], PS[64:65, 4:6, :], AF.Ln)
            for i in range(2):
                MM(PS[0:64, 6 + i, :], ones_f[64:65, 0:64], SCR[64:65, 4 + i, :], ms=(i == 1))
            BCS = SCR[0:64, 6:8, :]
            ACT(BCS, PS[0:64, 6:8, :], AF.Exp, scale=-1.0)
            return BCS

        def load_x(l, g):
            if l >= depth or g >= ngr:
                return
            src_d = xT_d if l == 0 else xmid_d
            DMA(XGs[(l * ngr + g) % 2][:], src_d.rearrange("(c p) t -> p c t", p=P)[:, :, g * NT:(g + 1) * NT],
                rk=('x', l, g))

        def group_step(l, g, src_d, dst_d):
            sl = slice(g * NT, (g + 1) * NT)
            lam_neg = SM[:, 10:11]
            gsub = SM[:, 11:12]
            arow = SM[:, 0:8]
            if l == 0:
                cast_upto(1, 3 * (g + 1) - 1)
            XG = XGs[(l * ngr + g) % 2]
            for rep in range(4):
                for tb in range(2):
                    DMA(ROPE[32 * rep:32 * rep + 32, tb, :], rope_d[tb, :, sl], rk=('rope', tb, g))
            norm_to_h(XG, l, 0, 0, prestat=not (l == 0 and g == 0))
            if l == 0 and g == 0:
                dump("h0", HY[:], [P, KC, NT])

            def wv(slot, kc, c0, n):
                return WR[:, slot, kc * 512 + c0:kc * 512 + c0 + n]

            def fm_tile(slot, c0):
                b = bank()
                for kc in range(KC):
                    MM(PS[:, b, :], wv(slot, kc, c0, 128), HY[:, kc, :], start=(kc == 0), stop=(kc == KC - 1))
                return b

            s0 = need(seqof(l, g, 0))
            for t in range(4):
                b = fm_tile(s0, t * 128)
                if t < 2:
                    CP(evac(), QTA[:, t, :], PS[:, b, :])
                else:
                    CP(evac(), KTA[:, t - 2, sl], PS[:, b, :])
            s1 = need(seqof(l, g, 1))
            for c in range(2):
                b = fm_tile(s1, c * 128)
                CP('dve', SCR[:, c, :], PS[:, b, :])
                ACT(SQ[:, c, :], SCR[:, c, :], AF.Square)
            b = fm_tile(s1, 256)
            CP('dve', SCR[:, 2, :], PS[:, b, :])
            ACT(SQ[:, 2, :], SCR[:, 2, :], AF.Square)
            bA = fm_tile(s1, 384)
            b = bank()
            for c in range(2):
                MM(PS[:, b, :], ones_bf, SQ[:, c, :], start=(c == 0), stop=(c == 1))
            rstd_from(PS[:, b, :], RS[:, 1, :], 256, epsR)
            for c in range(2):
                STT(CQN[:, c, :], SCR[:, c, :], SPT[:, l, 111 + c:112 + c], RS[:, 1, :], ALU.mult, ALU.mult)
            b = bank()
            MM(PS[:, b, :], ones_bf, SQ[:, 2, :])
            rstd_from(PS[:, b, :], RS[:, 0, :], 128, epsR)
            STT(CKVN[:], SCR[:, 2, :], SPT[:, l, 113:114], RS[:, 0, :], ALU.mult, ALU.mult)
            s2 = need(seqof(l, g, 2))
            bB = fm_tile(s2, 0)
            TT('dve', SCR[:, 3, :], PS[:, bA, :], ROPE[:, 0, :], ALU.mult)
            TT('dve', SCR[:, 4, :], PS[:, bB, :], ROPE[:, 1, :], ALU.mult)
            TT('pool', KRB[:, sl], SCR[:, 3, :], SCR[:, 4, :], ALU.add)
            for tt in range(4):
                b = bank()
                for kc in range(KC):
                    MM(PS[:, b, 0:256], HY[:, kc, tt * 128:(tt + 1) * 128], wv(s2, kc, 128, 256),
                       start=(kc == 0), stop=(kc == KC - 1))
                CP(evac(), vview(VPA, g * 4 + tt, 1)[:, 0, :, 0:64], PS[:, b, 0:256].rearrange("p (h e) -> p h e", h=4))
            b = bank()
            for tt in range(4):
                for kc in range(KC):
                    MM(PS[:, b, tt * 8:tt * 8 + 8], HY[:, kc, tt * 128:(tt + 1) * 128], wv(s2, kc, 384, 8),
                       start=(kc == 0), stop=(kc == KC - 1))
            DTR = DTS[:, 0, :].rearrange("p (t h) -> p t h", t=4)
            DT = DTS[:, 1, :].rearrange("p (t h) -> p t h", t=4)
            DTA = DTS[:, 2, :].rearrange("p (t h) -> p t h", t=4)
            TT('dve', DTR, PS[:, b, 0:32].rearrange("p (t h) -> p t h", t=4),
               _fr(RPT[:, 0:8], [[0, 4], [1, 8]]), ALU.add)
            ACT(DTR, DTR, AF.Exp)
            ACT(DT, DTR, AF.Ln, bias=onec)
            TT('dve', DTA, DT, _fr(arow, [[0, 4], [1, 8]]), ALU.mult)

            for t in range(4):
                b = bank()
                for kc in range(2):
                    MM(PS[:, b, :], WUQ[:, kc * 512 + t * 128:kc * 512 + t * 128 + 128], CQN[:, kc, :],
                       start=(kc == 0), stop=(kc == 1))
                if t < 2:
                    CP(evac(), QNB[:, t, :], PS[:, b, :])
                elif t == 2:
                    TT('dve', SCR[:, 3, :], PS[:, b, :], ROPE[:, 0, :], ALU.mult)
                else:
                    TT('dve', SCR[:, 4, :], PS[:, b, :], ROPE[:, 1, :], ALU.mult)
                    TT('pool', QRB[:], SCR[:, 3, :], SCR[:, 4, :], ALU.add)
            for t in range(2):
                b = bank()
                MM(PS[:, b, :], WUKV[:, t * 128:(t + 1) * 128], CKVN[:])
                CP(evac(), KNB[:, t, sl], PS[:, b, :])
            for tt in range(4):
                b = bank()
                MM(PS[:, b, 0:256], CKVN[:, tt * 128:(tt + 1) * 128], WUKV[:, 256:512])
                CP(evac(), vview(VPB, g * 4 + tt, 1)[:, 0, :, 0:64], PS[:, b, 0:256].rearrange("p (h e) -> p h e", h=4))
            def conv_tile(ct, b):
                xb = XB[:, ct % 2, :]
                acc = SCR[:, 6 + (ct % 2), :]
                CP('act', xb[:, 3:515], PS[:, b, :])
                CP('pool', xb[:, 0:3], HALO[:, ct, 0:3])
                cw = lambda j: SPT[:, l, 80 + ct * 4 + j:81 + ct * 4 + j]
                TS('dve', acc, xb[:, 0:512], cw(0), ALU.mult)
                for j in range(1, 4):
                    STT(acc, xb[:, j:j + 512], cw(j), acc, ALU.mult, ALU.add)
                CP('pool', HALO[:, ct, 0:3], xb[:, 512:515])
                dst = XST[:, ct, :] if ct < 4 else BCT[:, ct - 4, :]
                ACT(dst, acc, AF.Silu, bias=SPT[:, l, 104 + ct:105 + ct])

            s3 = need(seqof(l, g, 3))
            for ct in range(4):
                conv_tile(ct, fm_tile(s3, ct * 128))
            s4 = need(seqof(l, g, 4))
            for ct in range(4, 6):
                conv_tile(ct, fm_tile(s4, (ct - 4) * 128))
            s5 = need(seqof(l, g, 5))
            for tt in range(4):
                b = bank()
                for kc in range(KC):
                    MM(PS[:, b, :], HY[:, kc, tt * 128:(tt + 1) * 128], wv(s5, kc, 0, 512),
                       start=(kc == 0), stop=(kc == KC - 1))
                ACT(ZS[:, tt, :], PS[:, b, :], AF.Silu)

            if l == 0 and g == 0:
                dump("qta", QTA[:], [P, 2, NT])
                dump("qnb", QNB[:], [P, 2, NT])
                dump("qrb", QRB[:], [P, NT])
                dump("vpa", vview(VPA, 0, 4), [P, 4, 4, 65])

            QZ = ROPE[:].bitcast(BF16).rearrange("p a (r n) -> p (a r) n", n=NT)
            for j in range(2):
                O1 = SCR[:, 0, :]
                O2 = SCR[:, 1, :]
                OD = SCR[:, 2, :]
                MS('pool', QZ, 0.0)
                for r in range(4):
                    CP('pool' if r % 2 else 'dve', QZ[32 * r:32 * r + 32, r, :], QTA[32 * r:32 * r + 32, j, :])
                for hl in range(2):
                    def qk_a(kt, c0, sb_, j=j, hl=hl):
                        for i in range(2):
                            MM(PS[:, sb_ + i, c0:NT], KTA[:, j, kt * 128:(kt + 1) * 128],
                               QZ[:, 2 * hl + i, c0:NT], ms=(i == 1))

                    def vp_a(kt, i, j=j, hl=hl):
                        h = 2 * j + hl
                        return VPA[:, (kt * 4 + h) * 65:(kt * 4 + h) * 65 + 128]
                    BCS = attn_pass(g, qk_a, vp_a, 32 ** -0.5)
                    TT('dve', O1[64 * hl:64 * hl + 64, :], PS[0:64, 4, :], BCS[:, 0, :], ALU.mult)
                    TT('dve', O2[64 * hl:64 * hl + 64, :], PS[0:64, 5, :], BCS[:, 1, :], ALU.mult)
                STT(OD, O2, lam_neg, O1, ALU.mult, ALU.add)
                ACT(YN[:], OD, AF.Square)
                MM(PS[:, 6, :], blk_bf, YN[:])
                rstd_from(PS[:, 6, :], RS[:, 1, :], 64, epsS)
                STT(HY[:, j, :], OD, gsub, RS[:, 1, :], ALU.mult, ALU.mult)

            def qk_b(kt, c0):
                parts = ((KNB[:, 0, :], QNB[:, 0, :]), (KNB[:, 1, :], QNB[:, 1, :]), (KRB[:], QRB[:]))
                for pi, (kk, qq) in enumerate(parts):
                    for h in range(4):
                        MM(PS[:, h, c0:NT], kk[32 * h:32 * h + 32, kt * 128:(kt + 1) * 128],
                           qq[32 * h:32 * h + 32, c0:NT], start=(pi == 0), stop=(pi == 2),
                           ms=(pi == 2 and h == 3), tp=((96, 0) if h == 3 else None))

            def vp_b(kt, r):
                return VPB[:, (kt * 4 + r) * 65:(kt * 4 + r) * 65 + 128]
            BCS = attn_core(g, qk_b, vp_b, 96 ** -0.5)
            for h in range(4):
                TT('dve', HY[64 * (h % 2):64 * (h % 2) + 64, 2 + h // 2, :], PS[0:64, 4 + h, :], BCS[:, h, :],
                   ALU.mult)
            if l == 0 and g == 0:
                dump("yab", HY[:, 0:4, :], [P, 4, NT])

            v8 = lambda a: a.rearrange("p (h e) -> p h e", h=8)
            for tt in range(4):
                pb = PS[:, tt, :].bitcast(BF16)
                for ti in range(4):
                    TR(pb[:, ti * 128:(ti + 1) * 128], XST[:, ti, tt * 128:(tt + 1) * 128])
                CP(evac(), XTOK[:, tt, :], pb[:, 0:512])
                pb = PS[:, 4 + tt, :].bitcast(BF16)
                TR(pb[:, 0:128], BCT[:, 0, tt * 128:(tt + 1) * 128])
                CP(evac(), _fr(BTOK[:, tt, 0:1], [[128, 2], [1, 64]]),
                   pb[:, 0:128].rearrange("p (g n) -> p g n", g=2))
            if l == 0 and g == 0:
                dump("xtok", XTOK, [P, 4, NT])
                dump("dt", DTS[:, 1, :], [P, 32])
                dump("zs", ZS[:], [P, 4, NT])
                dump("bct", BCT[:], [P, 2, NT])
            MM(PS[:, 7, 0:32], Lm, DTS[:, 2, :])
            MM(PS[:, 7, 32:64], ones_f, DTS[:, 2, :])
            ACS = DTS[:, 4:6, :].rearrange("p a b -> p (a b)")
            CP('dve', ACS, PS[:, 7, 0:64])
            EE4 = DTS[:, 3, :]
            DTE4 = DTS[:, 6, :]
            DECG4 = DTS[:, 7, 0:16]
            ACT(EE4, ACS[:, 0:32], AF.Exp)
            TT('dve', DTE4, ACS[:, 32:64], ACS[:, 0:32], ALU.subtract)
            ACT(DTE4, DTE4, AF.Exp)
            for gg in range(2):
                ACT(DECG4[64 * gg:64 * gg + 64, :].rearrange("p (t k) -> p t k", t=4),
                    ACS[64 * gg:64 * gg + 64, 32:64].rearrange("p (t h) -> p t h", t=4)[:, :, 4 * gg:4 * gg + 4],
                    AF.Exp)
            XD4 = PTf[:, 1024:3072].rearrange("p (t n) -> p t n", t=4)
            TT('pool', PTf[:, 1024:3072].rearrange("p (q e) -> p q e", e=64),
               XTOK.rearrange("p t (h e) -> p (t h) e", e=64), _fr(DTS[:, 1, 0:1], [[1, 32], [0, 64]]), ALU.mult)
            Rb = ROPE[:].bitcast(BF16)
            XBb = XB[:].bitcast(BF16)
            hl8 = lambda a: a.rearrange("p (h l) -> p h l", h=8)
            ESEGs = [hl8(PTf[:, 0:1024]), QTA[:].rearrange("p a (h l) -> p (a h) l", l=128),
                     hl8(Rb[:, 0, :]), hl8(Rb[:, 1, :])]
            XDDs = [XBb[:, 0, 0:512], XBb[:, 0, 512:1024], XBb[:, 1, 0:512], XBb[:, 1, 512:1024]]
            gl2 = lambda a: a.rearrange("p (g l) -> p g l", g=2)
            GMs = [gl2(PTf[:, 3584:3840]), gl2(QRB[:, 0:256]), gl2(QRB[:, 256:512]), gl2(QNB[:, 0, 0:256])]
            YNs = [YN[:], CKVN[:]]
            def rlm_build(tt):
                pr_ = tt % 2
                RLM = SCR[:, 2 * pr_:2 * pr_ + 2, :].rearrange("p a (h l) -> p (a h) l", l=128)
                TT('dve', RLM, _fr(Lm, [[0, 8], [1, 128]]), _fr(DTA[:, tt, :], [[1, 8], [0, 128]]), ALU.mult)

            for tt in range(4):
                pr = tt % 2
                csl = slice(tt * 128, (tt + 1) * 128)
                dta = DTA[:, tt, :]
                DTE = DTE4[:, tt * 8:(tt + 1) * 8]
                ESG, XDD, GMp = ESEGs[tt], XDDs[tt], GMs[tt]
                sb_ = 2 if pr == 0 else 5
                if tt == 0:
                    rlm_build(0)
                for hf in range(2):
                    MM(PS[:, sb_ + hf, :], Umat, SCR[:, 2 * pr + hf, :], ms=(hf == 1))
                if tt + 1 < 4:
                    rlm_build(tt + 1)
                ACT(ESG, PS[:, sb_:sb_ + 2, :].rearrange("p a (h l) -> p (a h) l", l=128), AF.Exp)
                MM(PS[:, 1, 0:128], BCT[0:64, 0, csl], BCT[0:64, 1, csl], ms=False)
                MM(PS[:, 7, 256:384], BCT[64:128, 0, csl], BCT[64:128, 1, csl], ms=True)
                TT('dve', GMp, _fr(PS[:, 1, 0:1], [[6 * 512 + 256, 2], [1, 128]]), _fr(Lm, [[0, 2], [1, 128]]),
                   ALU.mult)
                E4 = ESG.rearrange("p (g k) l -> p g k l", g=2)
                TT('dve', E4, E4, _fr(GMp[:, 0, 0:1], [[128, 2], [0, 4], [1, 128]]), ALU.mult)
                TT('pool', v8(XDD), v8(XD4[:, tt, :]), _fr(DTE[:, 0:1], [[1, 8], [0, 64]]), ALU.mult)
            for tt in range(4):
                pr = tt % 2
                csl = slice(tt * 128, (tt + 1) * 128)
                EE = EE4[:, tt * 8:(tt + 1) * 8]
                DECG = DECG4[:, tt * 4:(tt + 1) * 4]
                ESG, XDD, YNp = ESEGs[tt], XDDs[tt], YNs[pr]
                SKp = SCR[:, 4 + pr, :]
                T2 = SCR[:, 6 + pr, :]
                SSQ = DTS[:, 7, 16 + 4 * pr:18 + 4 * pr]
                RSD = DTS[:, 7, 18 + 4 * pr:20 + 4 * pr]
                TT('pool', v8(SKp), v8(XTOK[:, tt, :]), _fr(RPT[:, 16:17], [[1, 8], [0, 64]]), ALU.mult)
                for h in range(8):
                    MM(PS[:, 4, h * 64:(h + 1) * 64], ESG[:, h, :], XD4[:, tt, h * 64:(h + 1) * 64], ms=(h == 7))
                for gg in range(2):
                    MM(PS[:, 5 + gg, gg * 256:(gg + 1) * 256], BCT[64 * gg:64 * gg + 64, 1, csl],
                       STB[64 * gg:64 * gg + 64, :], ms=(gg == 1))
                MM(PS[:, 1, 256:512], BTOK[:, tt, 0:128], XDD[:, 0:256], start=True, stop=False, ms=False)
                MM(PS[:, 1, 256:512], BTOK[:, tt, 64:192], XDD[:, 256:512], start=False, stop=True)
                TT('dve', T2.rearrange("p (g k e) -> p g k e", g=2, k=4), _fr(PS[:, 5, 0:1], [[768, 2], [64, 4], [1, 64]]),
                   _fr(EE[:, 0:1], [[4, 2], [1, 4], [0, 64]]), ALU.mult)
                TT('dve', T2, PS[:, 4, :], T2, ALU.add)
                TT('dve', T2, T2, SKp, ALU.add)
                TT('dve', T2, T2, ZS[:, tt, :], ALU.mult)
                for gg in range(2):
                    ACT(SKp[:, gg * 256:(gg + 1) * 256], T2[:, gg * 256:(gg + 1) * 256], AF.Square,
                        accum=SSQ[:, gg:gg + 1])
                ACT(RSD, SSQ, AF.Ln, scale=1.0 / 256, bias=epsR)
                ACT(RSD, RSD, AF.Exp, scale=-0.5)
                for gg in range(2):
                    STT(YNp[:, gg * 256:(gg + 1) * 256], T2[:, gg * 256:(gg + 1) * 256], RSD[:, gg:gg + 1],
                        RPT[:, 24 + gg * 256:24 + (gg + 1) * 256], ALU.mult, ALU.mult)
                pb = PS[:, 0, :].bitcast(BF16)
                for ti in range(4):
                    TR(pb[:, ti * 128:(ti + 1) * 128], YNp[:, ti * 128:(ti + 1) * 128])
                CP(evac(), HY[:, 4:8, csl], pb[:, 0:512].rearrange("p (t n) -> p t n", t=4))
                S4 = ST32[:].rearrange("p (k e) -> p k e", k=4)
                TT('dve', S4, S4, _fr(DECG[:, 0:1], [[1, 4], [0, 64]]), ALU.mult)
                TT('dve', ST32[:], ST32[:], PS[:, 1, 256:512], ALU.add)
                CP('act', STB[:], ST32[:])
            if l == 0 and g == 0:
                dump("yall", HY[:], [P, KC, NT])

            bss = 7
            for m in range(8):
                sw = need(seqof(l, g, 6 + m // 4))
                b = bank(0, 7)
                for kc in range(KC):
                    MM(PS[:, b, :], wv(sw, kc, (m % 4) * 128, 128), HY[:, kc, :], start=(kc == 0), stop=(kc == KC - 1))
                if m > 0:
                    MM(PS[:, bss, :], ones_bf, SQ[:, m - 1, :], start=(m == 1), stop=False, ms=True)
                CP('dve', SCR[:, m, :], PS[:, b, :])
                ACT(SQ[:, m, :], SCR[:, m, :], AF.Square)
            MM(PS[:, bss, :], ones_bf, SQ[:, 7, :], start=False, stop=True, ms=True)
            rstd_from(PS[:, bss, :], RS[:, 0, :], D, epsR)
            for m in range(8):
                STT(SCR[:, m, :], SCR[:, m, :], DER[:, l, 8 + m:9 + m], RS[:, 0, :], ALU.mult, ALU.mult)
                TT('pool' if m % 2 else 'dve', XG[:, m, :], XG[:, m, :], SCR[:, m, :], ALU.add)
                ACT(SQ[:, m, :], XG[:, m, :], AF.Square)
            if l == 0 and g == 0:
                dump("xmix", XG[:], [P, KC, NT])

            if g + 1 < ngr:
                load_x(l, g + 1)
            else:
                load_x(l + 1, 0)
            norm_to_h(XG, l, 16, 24, presq=True)
            has_next = not (l == depth - 1 and g == ngr - 1)
            XN = XGs[(l * ngr + g + 1) % 2]
            for j in range(8):
                sw = need(seqof(l, g, 8 + j))
                for f in range(4):
                    b = bank(0, 7)
                    for kc in range(KC):
                        MM(PS[:, b, :], wv(sw, kc, f * 128, 128), HY[:, kc, :], start=(kc == 0), stop=(kc == KC - 1))
                    a = AT[4 * j + f]
                    ACT(a, PS[:, b, :], AF.Relu)
                    TT('pool' if f % 2 else 'dve', a, a, a, ALU.mult)
            for m in range(8):
                sw = need(seqof(l, g, 16 + m))
                b = bank(0, 6)
                if has_next:
                    tq = (YN[:], CKVN[:])[m % 2]
                    ACT(tq, XN[:, m, :], AF.Square)
                for kc in range(32):
                    MM(PS[:, b, :], WR[:, sw, kc * 128:(kc + 1) * 128], AT[kc], start=(kc == 0), stop=(kc == 31))
                if m > 0:
                    MM(PS[:, bss, :], ones_bf, HY[:, m - 1, :], start=(m == 1), stop=False, ms=True)
                if has_next:
                    MM(PS[:, 6, :], ones_bf, tq, start=(m == 0), stop=(m == 7), ms=True)
                CP('dve', YM[m], PS[:, b, :])
                ACT(HY[:, m, :], YM[m], AF.Square)
            MM(PS[:, bss, :], ones_bf, HY[:, 7, :], start=False, stop=True, ms=True)
            if has_next:
                rstd_from(PS[:, 6, :], RS[:, 1, :], D, epsR)
            rstd_from(PS[:, bss, :], RS[:, 0, :], D, epsR)
            for m in range(8):
                STT(YM[m], YM[m], DER[:, l, 24 + m:25 + m], RS[:, 0, :], ALU.mult, ALU.mult)
                TT('pool' if m % 2 else 'dve', XG[:, m, :], XG[:, m, :], YM[m], ALU.add)
            return DMA(dst_d.rearrange("(c p) t -> p c t", p=P)[:, :, sl], XG[:], wk=('x', l + 1, g))

        try:
            load_x(0, 0)
            for l in range(depth):
                layer_setup(l)
                src = xT_d if l == 0 else xmid_d
                dst = out_d if l == depth - 1 else xmid_d
                for g in range(ngr):
                    tok = group_step(l, g, src, dst)
                    if l == depth - 1:
                        final_toks.append(tok)
        except _Stop:
            final_toks.append(DMA(out_d[0:P, 0:NT], RS[:, 0, :], wk=('x', 99, 0)))
        for tok in final_toks:
            S.wait_tok('sp', tok)
        build.stats = dict(nops=S.nops, nsem=S.nsem, sbuf_left=nc.sbuf_bytes_remaining)
    return nc


_NC_CACHE = {}


def _run(inputs, ngr=NG, depth=DEPTH, dbg=False, stop=99):
    shared = _prep_shared(inputs)
    x = np.asarray(inputs['x'], np.float32)
    c = np.asarray(inputs['c'], np.float32)
    pos = np.asarray(inputs['positions'], np.int32)
    nb = x.shape[0]
    in_maps = []
    for b in range(nb):
        m = dict(shared)
        m['xT'] = np.ascontiguousarray(x[b].T)
        m['pos'] = np.ascontiguousarray(pos[b].reshape(1, T))
        m['cvec'] = np.ascontiguousarray(c[b].reshape(KC, P).T)
        in_maps.append(m)
    key = (ngr, depth, dbg, stop)
    if key not in _NC_CACHE:
        _NC_CACHE[key] = build(ngr, depth, dbg, stop)
    nc = _NC_CACHE[key]
    res = run_bass_kernel_spmd(nc, in_maps, core_ids=list(range(nb)))
    return res


def kernel(**inputs):
    res = _run(inputs)
    out = np.stack([np.asarray(r['outT']).T for r in res.results], axis=0)
    return np.ascontiguousarray(out.astype(np.float32))
```

```python
import math
from contextlib import ExitStack

import numpy as np
import ml_dtypes

import concourse.bass as bass
import concourse.mybir as mybir
from concourse.ap import AP
from concourse.bass_utils import run_bass_kernel_spmd

F32 = mybir.dt.float32
BF16 = mybir.dt.bfloat16
I32 = mybir.dt.int32
AF = mybir.ActivationFunctionType
ALU = mybir.AluOpType

P = 128
D = 1024
KC = 8
T = 4096
NT = 512
NG = T // NT
DEPTH = 2
NPIECE = 24
NSP = 114
NRP = 664
RMS_EPS = 1e-6
SUBLN_EPS = 1e-5
TWO_PI = 2.0 * math.pi
CW1 = 6.28125
CW2 = TWO_PI - CW1
MAGIC = 12582912.0


def _box(a):
    if isinstance(a, tuple):
        return (a, 0, 1, 0, 1)
    shp = a.tensor.shape
    rowlen = 1
    for s in shp[1:]:
        rowlen *= s
    es = mybir.dt.size(a.dtype)
    off = a.offset
    aps = a.ap
    p0 = off // rowlen
    f0 = off % rowlen
    ext = 1
    for st, cn in aps[1:]:
        ext += (cn - 1) * abs(st)
    lo, hi = f0 * es, (f0 + ext) * es
    if a.tensor.name == "PS":
        return ("PS", 0, P, (lo // 2048) * 2048, ((hi + 2047) // 2048) * 2048)
    return (a.tensor.name, p0, p0 + aps[0][1], lo, hi)


class Sched:
    MAXV = 12000

    def __init__(self, nc, es):
        self.nc, self.es = nc, es
        self.eng = {'pe': nc.tensor, 'act': nc.scalar, 'dve': nc.vector, 'pool': nc.gpsimd,
                    'sp': nc.sync}
        self.nsem = 0
        self.csem, self.ccnt = {}, {}
        for e in ('pe', 'act', 'dve', 'pool'):
            self._rot(e)
        self.seen = {e: {} for e in self.eng}
        self.recs = {}
        self.dsem, self.dnext = {}, {}
        for q, k in (('sp', 24), ('pool', 8), ('act', 6)):
            self.dsem[q] = [[self._new("d%s%d" % (q, i)), 0] for i in range(k)]
            self.dnext[q] = 0
        self.nops = 0

    def _new(self, name):
        self.nsem += 1
        return self.es.enter_context(self.nc.semaphore(name))

    def _rot(self, e):
        self.csem[e] = self._new("c%s%d" % (e, self.nsem))
        self.ccnt[e] = 0

    def _deps(self, rb, wb, e=None):
        out = []
        for b in rb:
            ps = (b[0] == "PS")
            for r in self.recs.get(b[0], ()):
                if (r[5] or (ps and r[4][2] != e)) and r[0] < b[2] and b[1] < r[1] and r[2] < b[4] and b[3] < r[3]:
                    out.append(r[4])
        for b in wb:
            for r in self.recs.get(b[0], ()):
                if r[0] < b[2] and b[1] < r[1] and r[2] < b[4] and b[3] < r[3]:
                    out.append(r[4])
        return out

    def _rec(self, b, tok, isw):
        lst = self.recs.setdefault(b[0], [])
        p0, p1, lo, hi = b[1], b[2], b[3], b[4]
        if isw:
            lst[:] = [r for r in lst
                      if not (p0 <= r[0] and r[1] <= p1 and lo <= r[2] and r[3] <= hi)]
        else:
            te = tok[2]
            if te != 'dma':
                lst[:] = [r for r in lst
                          if not ((not r[5]) and r[4][2] == te and p0 <= r[0] and r[1] <= p1
                                  and lo <= r[2] and r[3] <= hi)]
        lst.append((p0, p1, lo, hi, tok, isw))

    def _wait(self, e, tok):
        sem, val, te = tok
        if te == e and e == 'pe':
            return
        if self.seen[e].get(sem.num, 0) >= val:
            return
        if te == 'pe' and sem is self.csem['pe'] and val > self.ccnt['pe']:
            raise RuntimeError("wait on a future PE milestone (mark the matmul ms=True)")
        self.eng[e].wait_ge(sem, val)
        self.seen[e][sem.num] = val

    def op(self, e, fn, reads=(), writes=(), ms=True):
        rb = [_box(a) for a in reads]
        wb = [_box(a) for a in writes]
        for t in self._deps(rb, wb, e):
            self._wait(e, t)
        ins = fn()
        self.nops += 1
        if e == 'pe' and not ms:
            tok = (self.csem[e], self.ccnt[e] + 1, e)
        else:
            self.ccnt[e] += 1
            ins.then_inc(self.csem[e], 1)
            tok = (self.csem[e], self.ccnt[e], e)
        for b in rb:
            self._rec(b, tok, False)
        for b in wb:
            self._rec(b, tok, True)
        if ms and self.ccnt[e] >= self.MAXV:
            self._rot(e)
        return tok

    def dma(self, q, out, in_, reads=(), writes=()):
        rb = [_box(a) for a in reads]
        wb = [_box(a) for a in writes]
        for t in self._deps(rb, wb, q):
            self._wait(q, t)
        slot = self.dsem[q][self.dnext[q]]
        self.dnext[q] = (self.dnext[q] + 1) % len(self.dsem[q])
        if slot[1]:
            self._wait(q, (slot[0], 16 * slot[1], 'dma'))
        ins = self.eng[q].dma_start(out=out, in_=in_)
        ins.then_inc(slot[0], 16)
        slot[1] += 1
        tok = (slot[0], 16 * slot[1], 'dma')
        for b in rb:
            self._rec(b, tok, False)
        for b in wb:
            self._rec(b, tok, True)
        return tok

    def wait_tok(self, e, tok):
        self._wait(e, tok)


def _fr(a, dims):
    return AP(a.tensor, a.offset, [list(a.ap[0])] + [list(d) for d in dims])


def _colpiece(M):
    return np.ascontiguousarray(M.reshape(KC, P, 512).transpose(1, 0, 2).reshape(P, 4096))


def _prep_pieces(w_in, w_out, w_up, w_down):
    pcs = np.zeros((NPIECE, P, 4096), np.float32)
    z = lambda n: np.zeros((D, n), np.float32)
    kr = w_in[:, 1152:1184]
    krs = np.concatenate([kr[:, 16:32], kr[:, 0:16]], 1)
    mats = [
        np.concatenate([w_in[:, 0:256], w_in[:, 256:512]], 1),
        np.concatenate([w_in[:, 768:1024], w_in[:, 1024:1152], kr, kr, kr, kr], 1),
        np.concatenate([krs, krs, krs, krs, w_in[:, 512:768], w_in[:, 2464:2472], z(120)], 1),
        w_in[:, 1696:2208],
        np.concatenate([w_in[:, 2208:2336], w_in[:, 2336:2464], z(256)], 1),
        w_in[:, 1184:1696],
    ]
    for i, M in enumerate(mats):
        pcs[i] = _colpiece(np.ascontiguousarray(M))
    for j in range(2):
        pcs[6 + j] = _colpiece(np.ascontiguousarray(w_out[:, 512 * j:512 * (j + 1)]))
    for j in range(8):
        pcs[8 + j] = _colpiece(np.ascontiguousarray(w_up[:, 512 * j:512 * (j + 1)]))
    for m in range(8):
        pcs[16 + m] = np.ascontiguousarray(
            w_down[:, 128 * m:128 * (m + 1)].reshape(32, P, 128).transpose(1, 0, 2)).reshape(P, 4096)
    return pcs


def _prep_shared(inp):
    f = lambda k: np.asarray(inp[k], np.float32)
    w_in, w_out, w_up, w_down = f('w_in'), f('w_out'), f('w_up'), f('w_down')
    wp = np.zeros((DEPTH, NPIECE, P, 4096), np.float32)
    for l in range(DEPTH):
        wp[l] = _prep_pieces(w_in[l], w_out[l], w_up[l], w_down[l])
    wp = wp.reshape(DEPTH * NPIECE * P, 4096)
    norm_g, b_ada = f('norm_g'), f('b_ada')
    conv_w, conv_b = f('conv_w'), f('conv_b')
    sp = np.zeros((DEPTH, P, NSP), np.float32)
    rp = np.zeros((DEPTH, NRP), np.float32)
    wuq = np.zeros((DEPTH, P, 2 * 512), np.float32)
    wukv = np.zeros((DEPTH, P, 512), np.float32)
    for l in range(DEPTH):
        for i in range(4):
            sp[l, :, 8 * i:8 * i + 8] = norm_g[l, i].reshape(8, P).T
        sp[l, :, 32:80] = b_ada[l].reshape(48, P).T
        sp[l, :, 80:104] = conv_w[l].reshape(4, 6, P).transpose(2, 1, 0).reshape(P, 24)
        sp[l, :, 104:110] = conv_b[l].reshape(6, P).T
        sp[l, :, 110] = np.tile(f('diff_subln_g')[l], 2)
        sp[l, :, 111:113] = f('mla_q_norm_g')[l].reshape(2, P).T
        sp[l, :, 113] = f('mla_kv_norm_g')[l]
        rp[l, 0:8] = f('dt_bias')[l]
        rp[l, 8:16] = f('a_log')[l]
        rp[l, 16:24] = f('d_skip')[l]
        rp[l, 24:536] = f('ssm_norm_g')[l]
        rp[l, 536:664] = f('diff_lambda')[l].reshape(-1)
        wq = f('w_uq')[l].reshape(256, 4, 96)
        nope, rope = wq[:, :, 0:64], wq[:, :, 64:96]
        ropes = np.concatenate([rope[:, :, 16:32], rope[:, :, 0:16]], 2)
        tiles = [nope[:, :, 0:32].reshape(256, 128), nope[:, :, 32:64].reshape(256, 128),
                 rope.reshape(256, 128), ropes.reshape(256, 128)]
        wqP = np.concatenate(tiles, 1)
        wuq[l] = wqP.reshape(2, P, 512).transpose(1, 0, 2).reshape(P, 1024)
        wk = f('w_ukv')[l].reshape(128, 4, 128)
        kn, v = wk[:, :, 0:64], wk[:, :, 64:128]
        wukv[l] = np.concatenate([kn[:, :, 0:32].reshape(128, 128), kn[:, :, 32:64].reshape(128, 128),
                                  v.reshape(128, 256)], 1)
    cbf = np.zeros((P, 384), np.float32)
    cbf[:, 0:128] = np.eye(P)
    cbf[:, 128:256] = 1.0
    cbf[0:64, 256:320] = 1.0
    cbf[64:128, 320:384] = 1.0
    cbf = cbf.astype(ml_dtypes.bfloat16)
    cf = np.zeros((P, 392), np.float32)
    j = np.arange(P)
    cf[:, 0:128] = (j[:, None] > j[None, :])
    cf[:, 128:256] = (j[:, None] <= j[None, :])
    cf[:, 256:384] = 1.0
    invf = (np.float32(1.0) / np.power(np.float32(10000.0),
                                       np.arange(0, 32, 2, dtype=np.float32) / np.float32(32))).astype(np.float32)
    cf[:, 384] = np.tile(invf, 8)
    cf[:, 385] = 0.25
    cf[:, 386] = np.tile(np.concatenate([np.full(16, 0.5), np.zeros(16)]), 4)
    cf[:, 387] = math.pi / 2
    cf[:, 388] = np.tile(np.concatenate([np.full(16, math.pi), np.zeros(16)]), 4)
    cf[:, 389] = RMS_EPS
    cf[:, 390] = SUBLN_EPS
    cf[:, 391] = 1.0
    return dict(wada=np.ascontiguousarray(f('w_ada')), wpieces=wp, sp=sp, rp=rp, wuq=wuq, wukv=wukv,
                cbf=cbf, cf=cf)


class _Stop(Exception):
    pass


def build(ngr=NG, depth=DEPTH, dbg=False, stop=99):
    nc = bass.Bass("TRN2", target_bir_lowering=False)

    def dram(name, shape, dt, kind):
        return nc.dram_tensor(name, shape, dt, kind=kind).ap()

    xT_d = dram("xT", [D, T], F32, "ExternalInput")
    pos_d = dram("pos", [1, T], I32, "ExternalInput")
    cv_d = dram("cvec", [P, KC], F32, "ExternalInput")
    wada_d = dram("wada", [DEPTH, D, 6 * D], F32, "ExternalInput")
    wp_d = dram("wpieces", [DEPTH * NPIECE * P, 4096], F32, "ExternalInput")
    sp_d = dram("sp", [DEPTH, P, NSP], F32, "ExternalInput")
    rp_d = dram("rp", [DEPTH, NRP], F32, "ExternalInput")
    wuq_d = dram("wuq", [DEPTH, P, 1024], F32, "ExternalInput")
    wukv_d = dram("wukv", [DEPTH, P, 512], F32, "ExternalInput")
    cbf_d = dram("cbf", [P, 384], BF16, "ExternalInput")
    cf_d = dram("cf", [P, 392], F32, "ExternalInput")
    out_d = dram("outT", [D, T], F32, "ExternalOutput")
    wbf_d = dram("wbf", [DEPTH * NPIECE * P, 4096], BF16, "Internal")
    rope_d = dram("ropetab", [2, 32, T], F32, "Internal")
    xmid_d = dram("xmid", [D, T], F32, "Internal")
    final_toks = []

    es = ExitStack()
    with es:
        S = Sched(nc, es)

        def sb(name, shape, dt):
            return es.enter_context(nc.sbuf_tensor(name, shape, dt))

        NWR = 3
        PS = es.enter_context(nc.psum_tensor("PS", [P, 8, 512], F32))
        CBF = sb("CBF", [P, 384], BF16)
        CF = sb("CF", [P, 392], F32)
        SPT = sb("SPT", [P, DEPTH, NSP], F32)
        RPT = sb("RPT", [P, NRP], F32)
        MODT = sb("MODT", [P, DEPTH, 48], F32)
        DER = sb("DER", [P, DEPTH, 32], F32)
        SM = sb("SM", [P, 64], F32)
        COND = sb("COND", [P, KC], F32)
        WUQ = sb("WUQ", [P, 1024], BF16)
        WUKV = sb("WUKV", [P, 512], BF16)
        XGs = [sb("XG0", [P, KC, NT], F32), sb("XG1", [P, KC, NT], F32)]
        HY = sb("HY", [P, KC, NT], BF16)
        SQ = sb("SQ", [P, KC, NT], BF16)
        RS = sb("RS", [P, 2, NT], F32)
        SCR = sb("SCR", [P, 8, NT], F32)
        WR = sb("WR", [P, NWR, 4096], BF16)
        KTA = sb("KTA", [P, 2, T], BF16)
        VPA = sb("VPA", [P, 32 * 4 * 65 + 64], BF16)
        KNB = sb("KNB", [P, 2, T], BF16)
        KRB = sb("KRB", [P, T], BF16)
        VPB = sb("VPB", [P, 32 * 4 * 65 + 64], BF16)
        QTA = sb("QTA", [P, 2, NT], BF16)
        QNB = sb("QNB", [P, 2, NT], BF16)
        QRB = sb("QRB", [P, NT], BF16)
        CKVN = sb("CKVN", [P, NT], BF16)
        PT = sb("PT", [P, 2, 4, NT], BF16)
        ROPE = sb("ROPE", [P, 2, NT], F32)
        XB = sb("XB", [P, 2, 516], F32)
        HALO = sb("HALO", [P, 6, 4], F32)
        BCT = sb("BCT", [P, 2, NT], BF16)
        CQN = BCT
        ZS = sb("ZS", [P, 4, NT], BF16)
        BTOK = sb("BTOK", [P, 4, 192], BF16)
        DTS = sb("DTS", [P, 8, 32], F32)
        ST32 = sb("ST32", [P, 256], F32)
        STB = sb("STB", [P, 256], BF16)
        YN = sb("YN", [P, NT], BF16)

        ident = CBF[:, 0:128]
        ones_bf = CBF[:, 128:256]
        blk_bf = CBF[:, 256:384]
        Umat = CF[:, 0:128]
        Lm = CF[:, 128:256]
        ones_f = CF[:, 256:384]
        epsR = CF[:, 389:390]
        epsS = CF[:, 390:391]
        onec = CF[:, 391:392]

        XST = SQ[:, 0:4, :]
        XTOK = SQ[:, 4:8, :]
        PTf = PT[:].rearrange("p a b c -> p (a b c)")
        SCRb = SCR[:].bitcast(BF16)
        AT = ([SCRb[:, i // 2, (i % 2) * 512:(i % 2) * 512 + 512] for i in range(16)]
              + [PT[:, i // 4, i % 4, :] for i in range(8)] + [SQ[:, i, :] for i in range(8)])
        WRf = WR[:].bitcast(F32)
        ZSf = ZS[:].bitcast(F32)
        YM = [ROPE[:, 0, :], ROPE[:, 1, :],
              ZSf[:, 0:2, :].rearrange("p a b -> p (a b)"), ZSf[:, 2:4, :].rearrange("p a b -> p (a b)"),
              XB[:, 0, 0:512], XB[:, 1, 0:512],
              QTA[:].bitcast(F32).rearrange("p a b -> p (a b)"),
              QNB[:].bitcast(F32).rearrange("p a b -> p (a b)")]

        _bank = [0]

        def bank(lo=0, hi=8):
            b = _bank[0]
            if b < lo or b >= hi:
                b = lo
            _bank[0] = b + 1 if b + 1 < hi else lo
            return b

        def isap(x):
            return hasattr(x, 'tensor')

        def MM(out, lhsT, rhs, start=True, stop=True, ms=None, tp=None):
            kw = {'tile_position': tp} if tp is not None else {}
            S.op('pe', lambda: nc.tensor.matmul(out, lhsT=lhsT, rhs=rhs, start=start, stop=stop, **kw),
                 reads=[lhsT, rhs], writes=[out], ms=(stop if ms is None else ms))

        def TR(out, in_):
            S.op('pe', lambda: nc.tensor.transpose(out, in_, ident), reads=[in_, ident], writes=[out])

        def ACT(out, in_, func, scale=1.0, bias=None, accum=None):
            rd, wr, kw = [in_], [out], {}
            if isap(scale):
                rd.append(scale)
            if bias is not None:
                kw['bias'] = bias
                if isap(bias):
                    rd.append(bias)
            if accum is not None:
                kw['accum_out'] = accum
                wr.append(accum)
            S.op('act', lambda: nc.scalar.activation(out=out, in_=in_, func=func, scale=scale, **kw),
                 reads=rd, writes=wr)

        def TT(e, out, in0, in1, op):
            S.op(e, lambda: S.eng[e].tensor_tensor(out=out, in0=in0, in1=in1, op=op),
                 reads=[in0, in1], writes=[out])

        def TS(e, out, in0, s1, op0, s2=None, op1=None):
            rd = [in0] + [s for s in (s1, s2) if isap(s)]
            kw = {} if op1 is None else {'op1': op1}
            S.op(e, lambda: S.eng[e].tensor_scalar(out=out, in0=in0, scalar1=s1, scalar2=s2, op0=op0, **kw),
                 reads=rd, writes=[out])

        def STT(out, in0, sc, in1, op0, op1):
            rd = [in0, in1] + ([sc] if isap(sc) else [])
            S.op('dve', lambda: nc.vector.scalar_tensor_tensor(out=out, in0=in0, scalar=sc, in1=in1,
                                                               op0=op0, op1=op1), reads=rd, writes=[out])

        def CP(e, out, in_):
            if e == 'act':
                ACT(out, in_, AF.Copy)
            else:
                S.op(e, lambda: S.eng[e].tensor_copy(out=out, in_=in_), reads=[in_], writes=[out])

        def MS(e, out, val):
            S.op(e, lambda: S.eng[e].memset(out, val), writes=[out])

        def RCP(out, in_):
            S.op('dve', lambda: nc.vector.reciprocal(out=out, in_=in_), reads=[in_], writes=[out])

        def DMA(out, in_, rk=None, wk=None, q='sp'):
            rd = [in_] if rk is None else [rk]
            wr = [out] if wk is None else [wk]
            return S.dma(q, out, in_, reads=rd, writes=wr)

        _ev = [0]

        def evac():
            _ev[0] ^= 1
            return 'act' if _ev[0] else 'dve'

        def dump(name, ap, shape):
            if not dbg:
                return
            d = dram("dbg_" + name, list(shape), ap.dtype, "ExternalOutput")
            final_toks.append(DMA(d, ap, wk=('dbg', name)))

        def rstd_from(ps_ap, out_rs, nfeat, epscol):
            ACT(out_rs, ps_ap, AF.Ln, scale=1.0 / nfeat, bias=epscol)
            ACT(out_rs, out_rs, AF.Exp, scale=-0.5)

        DMA(CBF[:], cbf_d, rk=('in', 'cbf'))
        DMA(CF[:], cf_d, rk=('in', 'cf'))
        DMA(SPT[:], sp_d.rearrange("l p n -> p l n"), rk=('in', 'sp'))
        DMA(COND[:], cv_d, rk=('in', 'cv'))
        def vview(VP, kt0, nkt):
            return VP[:, kt0 * 260:(kt0 + nkt) * 260].rearrange("p (k h e) -> p k h e", h=4, e=65)
        for VP in (VPA, VPB):
            MS('pool', VP[:], 0.0)
            MS('pool', vview(VP, 0, 32)[:, :, :, 64:65], 1.0)
        MS('pool', BTOK[:], 0.0)

        _cast = [[1, 0]]

        def cast_upto(l, i):
            cur = _cast[0]
            while (cur[0], cur[1]) <= (l, min(i, NPIECE - 1)) and cur[0] < depth:
                r0 = (cur[0] * NPIECE + cur[1]) * P
                DMA(wbf_d[r0:r0 + P, :], wp_d[r0:r0 + P, :], rk=('in', 'wp'), wk=('wbf', cur[0], cur[1]), q='pool')
                cur[1] += 1
                if cur[1] == NPIECE:
                    cur[0] += 1
                    cur[1] = 0

        PRO = 9

        ACT(COND[:], COND[:], AF.Silu)
        RT = SCR[0:32, :, :]
        for g in range(ngr if PRO >= 4 else 0):
            posi = RT[:, 0, :].bitcast(I32)
            DMA(posi, AP(pos_d.tensor, g * NT, [[0, 32], [1, NT]]), rk=('in', 'pos'), q='act')
            CP('dve', RT[:, 1, :], posi)
            TS('dve', RT[:, 1, :], RT[:, 1, :], CF[0:32, 384:385], ALU.mult)
            for tb in range(2):
                TS('dve', RT[:, 2, :], RT[:, 1, :], 1.0 / TWO_PI, ALU.mult, CF[0:32, 385 + tb:386 + tb], ALU.add)
                TS('dve', RT[:, 2, :], RT[:, 2, :], MAGIC, ALU.add, MAGIC, ALU.subtract)
                STT(RT[:, 3, :], RT[:, 2, :], -CW1, RT[:, 1, :], ALU.mult, ALU.add)
                STT(RT[:, 3, :], RT[:, 2, :], -CW2, RT[:, 3, :], ALU.mult, ALU.add)
                TS('dve', RT[:, 3, :], RT[:, 3, :], CF[0:32, 387 + tb:388 + tb], ALU.add, math.pi, ALU.min)
                TS('dve', RT[:, 3, :], RT[:, 3, :], -math.pi, ALU.max)
                ACT(RT[:, 4 + tb, :], RT[:, 3, :], AF.Sin)
                DMA(rope_d[tb, :, g * NT:(g + 1) * NT], RT[:, 4 + tb, :], wk=('rope', tb, g), q='act')

        def mod_derive(l):
            STT(DER[:, l, 0:8], MODT[:, l, 8:16], 1.0, SPT[:, l, 0:8], ALU.add, ALU.mult)
            TT('dve', DER[:, l, 8:16], MODT[:, l, 16:24], SPT[:, l, 8:16], ALU.mult)
            STT(DER[:, l, 16:24], MODT[:, l, 32:40], 1.0, SPT[:, l, 16:24], ALU.add, ALU.mult)
            TT('dve', DER[:, l, 24:32], MODT[:, l, 40:48], SPT[:, l, 24:32], ALU.mult)

        _mod1 = [0]

        def mod_deferred(n):
            if depth < 2:
                return
            for _ in range(n):
                m = _mod1[0]
                if m >= 48:
                    return
                _mod1[0] += 1
                k = m % 3
                stg = SCR[:, 2 * k:2 * k + 2, :].rearrange("p a (c n) -> p (a c) n", n=128)
                DMA(stg, wada_v[1, :, :, m * 128:(m + 1) * 128], rk=('in', 'wada'))
                b = bank()
                for kc in range(KC):
                    MM(PS[:, b, 0:1], stg[:, kc, :], COND[:, kc:kc + 1], start=(kc == 0), stop=(kc == KC - 1))
                TT('dve', MODT[:, 1, m:m + 1], PS[:, b, 0:1], SPT[:, 1, 32 + m:33 + m], ALU.add)

        wada_v = wada_d.rearrange("l (kc p) n -> l p kc n", p=P)
        ri = 0
        for l in range(depth):
            for i in range(24):
                slot = ri % NWR
                ri += 1
                dst = WRf[:, slot, :].rearrange("p (kc n) -> p kc n", kc=KC)
                DMA(dst, wada_v[l, :, :, i * 256:(i + 1) * 256], rk=('in', 'wada'))
                for mt in range(2):
                    m = 2 * i + mt
                    for kc in range(KC):
                        MM(PS[:, 7, l * 48 + m:l * 48 + m + 1],
                           WRf[:, slot, kc * 256 + mt * 128:kc * 256 + mt * 128 + 128],
                           COND[:, kc:kc + 1], start=(kc == 0), stop=(kc == KC - 1))
                if l > 0:
                    continue
                stg = XGs[i % 2][:].rearrange("p a b -> p (a b)")
                outb = (PTf, SQ[:].rearrange("p a b -> p (a b)"), HY[:].rearrange("p a b -> p (a b)"))[i % 3]
                r0 = i * P
                DMA(stg, wp_d[r0:r0 + P, :], rk=('in', 'wp'))
                CP('act' if i % 2 else 'dve', outb, stg)
                DMA(wbf_d[r0:r0 + P, :], outb, wk=('wbf', 0, i), q='act')
            TT('dve', MODT[:, l, :], PS[:, 7, l * 48:(l + 1) * 48], SPT[:, l, 32:80], ALU.add)
            mod_derive(l)
        if PRO >= 3:
            dump("modt", MODT[:, 0:1, :], [P, 1, 48])

        order = [(l, g, i) for l in range(depth) for g in range(ngr) for i in range(NPIECE)]
        emitted = [0]
        ring0 = ri

        def need(seq):
            lim = min(seq + NWR - 1, len(order) - 1)
            while emitted[0] <= lim:
                n = emitted[0]
                l, g, i = order[n]
                if g == 0:
                    cast_upto(l, i + 6)
                r0 = (l * NPIECE + i) * P
                DMA(WR[:, (ring0 + n) % NWR, :], wbf_d[r0:r0 + P, :], rk=('wbf', l, i))
                emitted[0] += 1
            return (ring0 + seq) % NWR

        def seqof(l, g, i):
            return (l * ngr + g) * NPIECE + i

        def layer_setup(l):
            lam_init = 0.8 - 0.6 * math.exp(-0.3 * l)
            DMA(RPT[:], AP(rp_d.tensor, l * NRP, [[0, P], [1, NRP]]), rk=('in', 'rp'))
            ACT(SM[:, 0:8], RPT[:, 8:16], AF.Exp)
            TS('dve', SM[:, 0:8], SM[:, 0:8], -1.0, ALU.mult)
            TT('dve', SM[:, 16:48], RPT[:, 536:568], RPT[:, 568:600], ALU.mult)
            ACT(SM[:, 16:48], SM[:, 16:48], AF.Identity, accum=SM[:, 8:9])
            TT('dve', SM[:, 16:48], RPT[:, 600:632], RPT[:, 632:664], ALU.mult)
            ACT(SM[:, 16:48], SM[:, 16:48], AF.Identity, accum=SM[:, 9:10])
            ACT(SM[:, 8:10], SM[:, 8:10], AF.Exp)
            TT('dve', SM[:, 10:11], SM[:, 9:10], SM[:, 8:9], ALU.subtract)
            TS('dve', SM[:, 10:11], SM[:, 10:11], -lam_init, ALU.add)
            TS('dve', SM[:, 11:12], SPT[:, l, 110:111], 1.0 - lam_init, ALU.mult)
            stage = SCR[:, 0:3, :].rearrange("p a b -> p (a b)")
            DMA(stage[:, 0:1024], wuq_d[l], rk=('in', 'wuq'))
            DMA(stage[:, 1024:1536], wukv_d[l], rk=('in', 'wukv'))
            CP('dve', WUQ[:], stage[:, 0:1024])
            CP('dve', WUKV[:], stage[:, 1024:1536])
            MS('pool', HALO[:], 0.0)
            MS('pool', ST32[:], 0.0)
            MS('pool', STB[:], 0.0)

        def norm_to_h(XG, l, gcol, shcol, presq=False, prestat=False):
            rs = RS[:, 1, :] if prestat else RS[:, 0, :]
            if not prestat:
                if not presq:
                    ACT(SQ[:], XG[:], AF.Square)
                b = bank()
                for c in range(KC):
                    MM(PS[:, b, :], ones_bf, SQ[:, c, :], start=(c == 0), stop=(c == KC - 1))
                rstd_from(PS[:, b, :], rs, D, epsR)
            for c in range(KC):
                tmp = SCR[:, 6 + (c % 2), :]
                TT('dve', tmp, XG[:, c, :], rs, ALU.mult)
                ACT(HY[:, c, :], tmp, AF.Identity, scale=DER[:, l, gcol + c:gcol + c + 1],
                    bias=MODT[:, l, shcol + c:shcol + c + 1])

        def attn_core(g, qk_fn, vp, scale):
            nkt = 4 * g + 4

            def c0of(kt):
                d = kt - 4 * g
                return 128 * d if d > 0 else 0

            def qk_exp(kt):
                c0 = c0of(kt)
                qk_fn(kt, c0)
                buf = kt % 2
                ACT(PT[:, buf, :, c0:NT], PS[:, 0:4, c0:NT], AF.Exp, scale=scale)
                if kt - 4 * g >= 0:
                    MS('pool', PT[64:128, buf, :, c0:c0 + 64], 0.0)

            qk_exp(0)
            for kt in range(nkt):
                if kt + 1 < nkt:
                    qk_exp(kt + 1)
                c0 = c0of(kt)
                for r in range(4):
                    MM(PS[:, 4 + r, c0:NT], vp(kt, r), PT[:, kt % 2, r, c0:NT],
                       start=(kt == 0), stop=(kt == nkt - 1), ms=(r == 3))
            ACT(SCR[64:65, 4:8, :], PS[64:65, 4:8, :], AF.Ln)
            for r in range(4):
                MM(PS[0:64, r, :], ones_f[64:65, 0:64], SCR[64:65, 4 + r, :], ms=(r == 3))
            BCS = SCR[0:64, 4:8, :]
            ACT(BCS, PS[0:64, 0:4, :], AF.Exp, scale=-1.0)
            return BCS

        PTq = PT[:].rearrange("p a (b s) n -> p (a b) s n", s=2)

        def attn_pass(g, qk_fn, vp, scale):
            nkt = 4 * g + 4

            def c0of(kt):
                d = kt - 4 * g
                return 128 * d if d > 0 else 0

            def qk_exp(kt):
                c0 = c0of(kt)
                sb_ = 2 * (kt % 2)
                qk_fn(kt, c0, sb_)
                ACT(PTq[:, kt % 4, :, c0:NT], PS[:, sb_:sb_ + 2, c0:NT], AF.Exp, scale=scale)
                if kt - 4 * g >= 0:
                    MS('pool', PTq[64:128, kt % 4, :, c0:c0 + 64], 0.0)

            qk_exp(0)
            for kt in range(nkt):
                if kt + 1 < nkt:
                    qk_exp(kt + 1)
                c0 = c0of(kt)
                for i in range(2):
                    MM(PS[:, 4 + i, c0:NT], vp(kt, i), PTq[:, kt % 4, i, c0:NT],
                       start=(kt == 0), stop=(kt == nkt - 1), ms=(i == 1))
            ACT(SCR[64:65, 4:6, :], PS[64:65, 4:6, :], AF.Ln)
            for i in range(2):
                MM(PS[0:64, 6 + i, :], ones_f[64:65, 0:64], SCR[64:65, 4 + i, :], ms=(i == 1))
            BCS = SCR[0:64, 6:8, :]
            ACT(BCS, PS[0:64, 6:8, :], AF.Exp, scale=-1.0)
            return BCS

        def load_x(l, g):
            if l >= depth or g >= ngr:
                return
            src_d = xT_d if l == 0 else xmid_d
            DMA(XGs[(l * ngr + g) % 2][:], src_d.rearrange("(c p) t -> p c t", p=P)[:, :, g * NT:(g + 1) * NT],
                rk=('x', l, g))

        def group_step(l, g, src_d, dst_d):
            sl = slice(g * NT, (g + 1) * NT)
            lam_neg = SM[:, 10:11]
            gsub = SM[:, 11:12]
            arow = SM[:, 0:8]
            XG = XGs[(l * ngr + g) % 2]
            for rep in range(4):
                for tb in range(2):
                    DMA(ROPE[32 * rep:32 * rep + 32, tb, :], rope_d[tb, :, sl], rk=('rope', tb, g))
            norm_to_h(XG, l, 0, 0, prestat=not (l == 0 and g == 0))
            if l == 0 and g == 0:
                dump("h0", HY[:], [P, KC, NT])

            def wv(slot, kc, c0, n):
                return WR[:, slot, kc * 512 + c0:kc * 512 + c0 + n]

            def fm_tile(slot, c0):
                b = bank()
                for kc in range(KC):
                    MM(PS[:, b, :], wv(slot, kc, c0, 128), HY[:, kc, :], start=(kc == 0), stop=(kc == KC - 1))
                return b

            s0 = need(seqof(l, g, 0))
            for t in range(4):
                b = fm_tile(s0, t * 128)
                if t < 2:
                    CP(evac(), QTA[:, t, :], PS[:, b, :])
                else:
                    CP(evac(), KTA[:, t - 2, sl], PS[:, b, :])
            s1 = need(seqof(l, g, 1))
            for c in range(2):
                b = fm_tile(s1, c * 128)
                CP('dve', SCR[:, c, :], PS[:, b, :])
                ACT(SQ[:, c, :], SCR[:, c, :], AF.Square)
            b = fm_tile(s1, 256)
            CP('dve', SCR[:, 2, :], PS[:, b, :])
            ACT(SQ[:, 2, :], SCR[:, 2, :], AF.Square)
            bA = fm_tile(s1, 384)
            b = bank()
            for c in range(2):
                MM(PS[:, b, :], ones_bf, SQ[:, c, :], start=(c == 0), stop=(c == 1))
            rstd_from(PS[:, b, :], RS[:, 1, :], 256, epsR)
            for c in range(2):
                STT(CQN[:, c, :], SCR[:, c, :], SPT[:, l, 111 + c:112 + c], RS[:, 1, :], ALU.mult, ALU.mult)
            b = bank()
            MM(PS[:, b, :], ones_bf, SQ[:, 2, :])
            rstd_from(PS[:, b, :], RS[:, 0, :], 128, epsR)
            STT(CKVN[:], SCR[:, 2, :], SPT[:, l, 113:114], RS[:, 0, :], ALU.mult, ALU.mult)
            s2 = need(seqof(l, g, 2))
            bB = fm_tile(s2, 0)
            TT('dve', SCR[:, 3, :], PS[:, bA, :], ROPE[:, 0, :], ALU.mult)
            TT('dve', SCR[:, 4, :], PS[:, bB, :], ROPE[:, 1, :], ALU.mult)
            TT('pool', KRB[:, sl], SCR[:, 3, :], SCR[:, 4, :], ALU.add)
            for tt in range(4):
                b = bank()
                for kc in range(KC):
                    MM(PS[:, b, 0:256], HY[:, kc, tt * 128:(tt + 1) * 128], wv(s2, kc, 128, 256),
                       start=(kc == 0), stop=(kc == KC - 1))
                CP(evac(), vview(VPA, g * 4 + tt, 1)[:, 0, :, 0:64], PS[:, b, 0:256].rearrange("p (h e) -> p h e", h=4))
            b = bank()
            for tt in range(4):
                for kc in range(KC):
                    MM(PS[:, b, tt * 8:tt * 8 + 8], HY[:, kc, tt * 128:(tt + 1) * 128], wv(s2, kc, 384, 8),
                       start=(kc == 0), stop=(kc == KC - 1))
            DTR = DTS[:, 0, :].rearrange("p (t h) -> p t h", t=4)
            DT = DTS[:, 1, :].rearrange("p (t h) -> p t h", t=4)
            DTA = DTS[:, 2, :].rearrange("p (t h) -> p t h", t=4)
            TT('dve', DTR, PS[:, b, 0:32].rearrange("p (t h) -> p t h", t=4),
               _fr(RPT[:, 0:8], [[0, 4], [1, 8]]), ALU.add)
            ACT(DTR, DTR, AF.Exp)
            ACT(DT, DTR, AF.Ln, bias=onec)
            TT('dve', DTA, DT, _fr(arow, [[0, 4], [1, 8]]), ALU.mult)

            for t in range(4):
                b = bank()
                for kc in range(2):
                    MM(PS[:, b, :], WUQ[:, kc * 512 + t * 128:kc * 512 + t * 128 + 128], CQN[:, kc, :],
                       start=(kc == 0), stop=(kc == 1))
                if t < 2:
                    CP(evac(), QNB[:, t, :], PS[:, b, :])
                elif t == 2:
                    TT('dve', SCR[:, 3, :], PS[:, b, :], ROPE[:, 0, :], ALU.mult)
                else:
                    TT('dve', SCR[:, 4, :], PS[:, b, :], ROPE[:, 1, :], ALU.mult)
                    TT('pool', QRB[:], SCR[:, 3, :], SCR[:, 4, :], ALU.add)
            for t in range(2):
                b = bank()
                MM(PS[:, b, :], WUKV[:, t * 128:(t + 1) * 128], CKVN[:])
                CP(evac(), KNB[:, t, sl], PS[:, b, :])
            for tt in range(4):
                b = bank()
                MM(PS[:, b, 0:256], CKVN[:, tt * 128:(tt + 1) * 128], WUKV[:, 256:512])
                CP(evac(), vview(VPB, g * 4 + tt, 1)[:, 0, :, 0:64], PS[:, b, 0:256].rearrange("p (h e) -> p h e", h=4))
            def conv_tile(ct, b):
                xb = XB[:, ct % 2, :]
                acc = SCR[:, 6 + (ct % 2), :]
                CP('act', xb[:, 3:515], PS[:, b, :])
                CP('pool', xb[:, 0:3], HALO[:, ct, 0:3])
                cw = lambda j: SPT[:, l, 80 + ct * 4 + j:81 + ct * 4 + j]
                TS('dve', acc, xb[:, 0:512], cw(0), ALU.mult)
                for j in range(1, 4):
                    STT(acc, xb[:, j:j + 512], cw(j), acc, ALU.mult, ALU.add)
                CP('pool', HALO[:, ct, 0:3], xb[:, 512:515])
                dst = XST[:, ct, :] if ct < 4 else BCT[:, ct - 4, :]
                ACT(dst, acc, AF.Silu, bias=SPT[:, l, 104 + ct:105 + ct])

            s3 = need(seqof(l, g, 3))
            for ct in range(4):
                conv_tile(ct, fm_tile(s3, ct * 128))
            s4 = need(seqof(l, g, 4))
            for ct in range(4, 6):
                conv_tile(ct, fm_tile(s4, (ct - 4) * 128))
            s5 = need(seqof(l, g, 5))
            for tt in range(4):
                b = bank()
                for kc in range(KC):
                    MM(PS[:, b, :], HY[:, kc, tt * 128:(tt + 1) * 128], wv(s5, kc, 0, 512),
                       start=(kc == 0), stop=(kc == KC - 1))
                ACT(ZS[:, tt, :], PS[:, b, :], AF.Silu)

            if l == 0 and g == 0:
                dump("qta", QTA[:], [P, 2, NT])
                dump("qnb", QNB[:], [P, 2, NT])
                dump("qrb", QRB[:], [P, NT])
                dump("vpa", vview(VPA, 0, 4), [P, 4, 4, 65])

            QZ = ROPE[:].bitcast(BF16).rearrange("p a (r n) -> p (a r) n", n=NT)
            for j in range(2):
                O1 = SCR[:, 0, :]
                O2 = SCR[:, 1, :]
                OD = SCR[:, 2, :]
                MS('pool', QZ, 0.0)
                for r in range(4):
                    CP('pool' if r % 2 else 'dve', QZ[32 * r:32 * r + 32, r, :], QTA[32 * r:32 * r + 32, j, :])
                for hl in range(2):
                    def qk_a(kt, c0, sb_, j=j, hl=hl):
                        for i in range(2):
                            MM(PS[:, sb_ + i, c0:NT], KTA[:, j, kt * 128:(kt + 1) * 128],
                               QZ[:, 2 * hl + i, c0:NT], ms=(i == 1))

                    def vp_a(kt, i, j=j, hl=hl):
                        h = 2 * j + hl
                        return VPA[:, (kt * 4 + h) * 65:(kt * 4 + h) * 65 + 128]
                    BCS = attn_pass(g, qk_a, vp_a, 32 ** -0.5)
                    TT('dve', O1[64 * hl:64 * hl + 64, :], PS[0:64, 4, :], BCS[:, 0, :], ALU.mult)
                    TT('dve', O2[64 * hl:64 * hl + 64, :], PS[0:64, 5, :], BCS[:, 1, :], ALU.mult)
                STT(OD, O2, lam_neg, O1, ALU.mult, ALU.add)
                ACT(YN[:], OD, AF.Square)
                MM(PS[:, 6, :], blk_bf, YN[:])
                rstd_from(PS[:, 6, :], RS[:, 1, :], 64, epsS)
                STT(HY[:, j, :], OD, gsub, RS[:, 1, :], ALU.mult, ALU.mult)

            def qk_b(kt, c0):
                parts = ((KNB[:, 0, :], QNB[:, 0, :]), (KNB[:, 1, :], QNB[:, 1, :]), (KRB[:], QRB[:]))
                for pi, (kk, qq) in enumerate(parts):
                    for h in range(4):
                        MM(PS[:, h, c0:NT], kk[32 * h:32 * h + 32, kt * 128:(kt + 1) * 128],
                           qq[32 * h:32 * h + 32, c0:NT], start=(pi == 0), stop=(pi == 2),
                           ms=(pi == 2 and h == 3), tp=((96, 0) if h == 3 else None))

            def vp_b(kt, r):
                return VPB[:, (kt * 4 + r) * 65:(kt * 4 + r) * 65 + 128]
            if l == 0:
                cast_upto(1, 3 * (g + 1) - 1)
            BCS = attn_core(g, qk_b, vp_b, 96 ** -0.5)
            for h in range(4):
                TT('dve', HY[64 * (h % 2):64 * (h % 2) + 64, 2 + h // 2, :], PS[0:64, 4 + h, :], BCS[:, h, :],
                   ALU.mult)
            if l == 0 and g == 0:
                dump("yab", HY[:, 0:4, :], [P, 4, NT])

            v8 = lambda a: a.rearrange("p (h e) -> p h e", h=8)
            for tt in range(4):
                pb = PS[:, tt, :].bitcast(BF16)
                for ti in range(4):
                    TR(pb[:, ti * 128:(ti + 1) * 128], XST[:, ti, tt * 128:(tt + 1) * 128])
                CP(evac(), XTOK[:, tt, :], pb[:, 0:512])
                pb = PS[:, 4 + tt, :].bitcast(BF16)
                TR(pb[:, 0:128], BCT[:, 0, tt * 128:(tt + 1) * 128])
                CP(evac(), _fr(BTOK[:, tt, 0:1], [[128, 2], [1, 64]]),
                   pb[:, 0:128].rearrange("p (g n) -> p g n", g=2))
            if l == 0 and g == 0:
                dump("xtok", XTOK, [P, 4, NT])
                dump("dt", DTS[:, 1, :], [P, 32])
                dump("zs", ZS[:], [P, 4, NT])
                dump("bct", BCT[:], [P, 2, NT])
            MM(PS[:, 7, 0:32], Lm, DTS[:, 2, :])
            MM(PS[:, 7, 32:64], ones_f, DTS[:, 2, :])
            ACS = DTS[:, 4:6, :].rearrange("p a b -> p (a b)")
            CP('dve', ACS, PS[:, 7, 0:64])
            EE4 = DTS[:, 3, :]
            DTE4 = DTS[:, 6, :]
            DECG4 = DTS[:, 7, 0:16]
            ACT(EE4, ACS[:, 0:32], AF.Exp)
            TT('dve', DTE4, ACS[:, 32:64], ACS[:, 0:32], ALU.subtract)
            ACT(DTE4, DTE4, AF.Exp)
            for gg in range(2):
                ACT(DECG4[64 * gg:64 * gg + 64, :].rearrange("p (t k) -> p t k", t=4),
                    ACS[64 * gg:64 * gg + 64, 32:64].rearrange("p (t h) -> p t h", t=4)[:, :, 4 * gg:4 * gg + 4],
                    AF.Exp)
            XD4 = PTf[:, 1024:3072].rearrange("p (t n) -> p t n", t=4)
            TT('pool', PTf[:, 1024:3072].rearrange("p (q e) -> p q e", e=64),
               XTOK.rearrange("p t (h e) -> p (t h) e", e=64), _fr(DTS[:, 1, 0:1], [[1, 32], [0, 64]]), ALU.mult)
            Rb = ROPE[:].bitcast(BF16)
            XBb = XB[:].bitcast(BF16)
            hl8 = lambda a: a.rearrange("p (h l) -> p h l", h=8)
            ESEGs = [hl8(PTf[:, 0:1024]), QTA[:].rearrange("p a (h l) -> p (a h) l", l=128),
                     hl8(Rb[:, 0, :]), hl8(Rb[:, 1, :])]
            XDDs = [XBb[:, 0, 0:512], XBb[:, 0, 512:1024], XBb[:, 1, 0:512], XBb[:, 1, 512:1024]]
            gl2 = lambda a: a.rearrange("p (g l) -> p g l", g=2)
            GMs = [gl2(PTf[:, 3584:3840]), gl2(QRB[:, 0:256]), gl2(QRB[:, 256:512]), gl2(QNB[:, 0, 0:256])]
            YNs = [YN[:], CKVN[:]]
            def rlm_build(tt):
                pr_ = tt % 2
                RLM = SCR[:, 2 * pr_:2 * pr_ + 2, :].rearrange("p a (h l) -> p (a h) l", l=128)
                TT('dve', RLM, _fr(Lm, [[0, 8], [1, 128]]), _fr(DTA[:, tt, :], [[1, 8], [0, 128]]), ALU.mult)

            for tt in range(4):
                pr = tt % 2
                csl = slice(tt * 128, (tt + 1) * 128)
                dta = DTA[:, tt, :]
                DTE = DTE4[:, tt * 8:(tt + 1) * 8]
                ESG, XDD, GMp = ESEGs[tt], XDDs[tt], GMs[tt]
                sb_ = 2 if pr == 0 else 5
                if tt == 0:
                    rlm_build(0)
                for hf in range(2):
                    MM(PS[:, sb_ + hf, :], Umat, SCR[:, 2 * pr + hf, :], ms=(hf == 1))
                if tt + 1 < 4:
                    rlm_build(tt + 1)
                ACT(ESG, PS[:, sb_:sb_ + 2, :].rearrange("p a (h l) -> p (a h) l", l=128), AF.Exp)
                MM(PS[:, 1, 0:128], BCT[0:64, 0, csl], BCT[0:64, 1, csl], ms=False)
                MM(PS[:, 7, 256:384], BCT[64:128, 0, csl], BCT[64:128, 1, csl], ms=True)
                TT('dve', GMp, _fr(PS[:, 1, 0:1], [[6 * 512 + 256, 2], [1, 128]]), _fr(Lm, [[0, 2], [1, 128]]),
                   ALU.mult)
                E4 = ESG.rearrange("p (g k) l -> p g k l", g=2)
                TT('dve', E4, E4, _fr(GMp[:, 0, 0:1], [[128, 2], [0, 4], [1, 128]]), ALU.mult)
                TT('pool', v8(XDD), v8(XD4[:, tt, :]), _fr(DTE[:, 0:1], [[1, 8], [0, 64]]), ALU.mult)
            for tt in range(4):
                pr = tt % 2
                csl = slice(tt * 128, (tt + 1) * 128)
                EE = EE4[:, tt * 8:(tt + 1) * 8]
                DECG = DECG4[:, tt * 4:(tt + 1) * 4]
                ESG, XDD, YNp = ESEGs[tt], XDDs[tt], YNs[pr]
                SKp = SCR[:, 4 + pr, :]
                T2 = SCR[:, 6 + pr, :]
                SSQ = DTS[:, 7, 16 + 4 * pr:18 + 4 * pr]
                RSD = DTS[:, 7, 18 + 4 * pr:20 + 4 * pr]
                TT('pool', v8(SKp), v8(XTOK[:, tt, :]), _fr(RPT[:, 16:17], [[1, 8], [0, 64]]), ALU.mult)
                for h in range(8):
                    MM(PS[:, 4, h * 64:(h + 1) * 64], ESG[:, h, :], XD4[:, tt, h * 64:(h + 1) * 64], ms=(h == 7))
                for gg in range(2):
                    MM(PS[:, 5 + gg, gg * 256:(gg + 1) * 256], BCT[64 * gg:64 * gg + 64, 1, csl],
                       STB[64 * gg:64 * gg + 64, :], ms=(gg == 1))
                MM(PS[:, 1, 256:512], BTOK[:, tt, 0:128], XDD[:, 0:256], start=True, stop=False, ms=False)
                MM(PS[:, 1, 256:512], BTOK[:, tt, 64:192], XDD[:, 256:512], start=False, stop=True)
                TT('dve', T2.rearrange("p (g k e) -> p g k e", g=2, k=4), _fr(PS[:, 5, 0:1], [[768, 2], [64, 4], [1, 64]]),
                   _fr(EE[:, 0:1], [[4, 2], [1, 4], [0, 64]]), ALU.mult)
                TT('dve', T2, PS[:, 4, :], T2, ALU.add)
                TT('dve', T2, T2, SKp, ALU.add)
                TT('dve', T2, T2, ZS[:, tt, :], ALU.mult)
                for gg in range(2):
                    ACT(SKp[:, gg * 256:(gg + 1) * 256], T2[:, gg * 256:(gg + 1) * 256], AF.Square,
                        accum=SSQ[:, gg:gg + 1])
                ACT(RSD, SSQ, AF.Ln, scale=1.0 / 256, bias=epsR)
                ACT(RSD, RSD, AF.Exp, scale=-0.5)
                for gg in range(2):
                    STT(YNp[:, gg * 256:(gg + 1) * 256], T2[:, gg * 256:(gg + 1) * 256], RSD[:, gg:gg + 1],
                        RPT[:, 24 + gg * 256:24 + (gg + 1) * 256], ALU.mult, ALU.mult)
                pb = PS[:, 0, :].bitcast(BF16)
                for ti in range(4):
                    TR(pb[:, ti * 128:(ti + 1) * 128], YNp[:, ti * 128:(ti + 1) * 128])
                CP(evac(), HY[:, 4:8, csl], pb[:, 0:512].rearrange("p (t n) -> p t n", t=4))
                S4 = ST32[:].rearrange("p (k e) -> p k e", k=4)
                TT('dve', S4, S4, _fr(DECG[:, 0:1], [[1, 4], [0, 64]]), ALU.mult)
                TT('dve', ST32[:], ST32[:], PS[:, 1, 256:512], ALU.add)
                CP('act', STB[:], ST32[:])
            if l == 0 and g == 0:
                dump("yall", HY[:], [P, KC, NT])

            bss = 7
            for m in range(8):
                sw = need(seqof(l, g, 6 + m // 4))
                b = bank(0, 7)
                for kc in range(KC):
                    MM(PS[:, b, :], wv(sw, kc, (m % 4) * 128, 128), HY[:, kc, :], start=(kc == 0), stop=(kc == KC - 1))
                if m > 0:
                    MM(PS[:, bss, :], ones_bf, SQ[:, m - 1, :], start=(m == 1), stop=False, ms=True)
                CP('dve', SCR[:, m, :], PS[:, b, :])
                ACT(SQ[:, m, :], SCR[:, m, :], AF.Square)
            MM(PS[:, bss, :], ones_bf, SQ[:, 7, :], start=False, stop=True, ms=True)
            rstd_from(PS[:, bss, :], RS[:, 0, :], D, epsR)
            for m in range(8):
                STT(SCR[:, m, :], SCR[:, m, :], DER[:, l, 8 + m:9 + m], RS[:, 0, :], ALU.mult, ALU.mult)
                TT('pool' if m % 2 else 'dve', XG[:, m, :], XG[:, m, :], SCR[:, m, :], ALU.add)
                ACT(SQ[:, m, :], XG[:, m, :], AF.Square)
            if l == 0 and g == 0:
                dump("xmix", XG[:], [P, KC, NT])

            if g + 1 < ngr:
                load_x(l, g + 1)
            else:
                load_x(l + 1, 0)
            norm_to_h(XG, l, 16, 24, presq=True)
            has_next = not (l == depth - 1 and g == ngr - 1)
            XN = XGs[(l * ngr + g + 1) % 2]
            for j in range(8):
                sw = need(seqof(l, g, 8 + j))
                for f in range(4):
                    b = bank(0, 7)
                    for kc in range(KC):
                        MM(PS[:, b, :], wv(sw, kc, f * 128, 128), HY[:, kc, :], start=(kc == 0), stop=(kc == KC - 1))
                    a = AT[4 * j + f]
                    ACT(a, PS[:, b, :], AF.Relu)
                    TT('pool' if f % 2 else 'dve', a, a, a, ALU.mult)
            for m in range(8):
                sw = need(seqof(l, g, 16 + m))
                b = bank(0, 6)
                if has_next:
                    tq = (YN[:], CKVN[:])[m % 2]
                    ACT(tq, XN[:, m, :], AF.Square)
                for kc in range(32):
                    MM(PS[:, b, :], WR[:, sw, kc * 128:(kc + 1) * 128], AT[kc], start=(kc == 0), stop=(kc == 31))
                if m > 0:
                    MM(PS[:, bss, :], ones_bf, HY[:, m - 1, :], start=(m == 1), stop=False, ms=True)
                if has_next:
                    MM(PS[:, 6, :], ones_bf, tq, start=(m == 0), stop=(m == 7), ms=True)
                CP('dve', YM[m], PS[:, b, :])
                ACT(HY[:, m, :], YM[m], AF.Square)
            MM(PS[:, bss, :], ones_bf, HY[:, 7, :], start=False, stop=True, ms=True)
            if has_next:
                rstd_from(PS[:, 6, :], RS[:, 1, :], D, epsR)
            rstd_from(PS[:, bss, :], RS[:, 0, :], D, epsR)
            for m in range(8):
                STT(YM[m], YM[m], DER[:, l, 24 + m:25 + m], RS[:, 0, :], ALU.mult, ALU.mult)
                TT('pool' if m % 2 else 'dve', XG[:, m, :], XG[:, m, :], YM[m], ALU.add)
            return DMA(dst_d.rearrange("(c p) t -> p c t", p=P)[:, :, sl], XG[:], wk=('x', l + 1, g))

        try:
            load_x(0, 0)
            for l in range(depth):
                layer_setup(l)
                src = xT_d if l == 0 else xmid_d
                dst = out_d if l == depth - 1 else xmid_d
                for g in range(ngr):
                    tok = group_step(l, g, src, dst)
                    if l == depth - 1:
                        final_toks.append(tok)
        except _Stop:
            final_toks.append(DMA(out_d[0:P, 0:NT], RS[:, 0, :], wk=('x', 99, 0)))
        for tok in final_toks:
            S.wait_tok('sp', tok)
        build.stats = dict(nops=S.nops, nsem=S.nsem, sbuf_left=nc.sbuf_bytes_remaining)
    return nc


_NC_CACHE = {}


def _run(inputs, ngr=NG, depth=DEPTH, dbg=False, stop=99):
    shared = _prep_shared(inputs)
    x = np.asarray(inputs['x'], np.float32)
    c = np.asarray(inputs['c'], np.float32)
    pos = np.asarray(inputs['positions'], np.int32)
    nb = x.shape[0]
    in_maps = []
    for b in range(nb):
        m = dict(shared)
        m['xT'] = np.ascontiguousarray(x[b].T)
        m['pos'] = np.ascontiguousarray(pos[b].reshape(1, T))
        m['cvec'] = np.ascontiguousarray(c[b].reshape(KC, P).T)
        in_maps.append(m)
    key = (ngr, depth, dbg, stop)
    if key not in _NC_CACHE:
        _NC_CACHE[key] = build(ngr, depth, dbg, stop)
    nc = _NC_CACHE[key]
    res = run_bass_kernel_spmd(nc, in_maps, core_ids=list(range(nb)))
    return res


def kernel(**inputs):
    res = _run(inputs)
    out = np.stack([np.asarray(r['outT']).T for r in res.results], axis=0)
    return np.ascontiguousarray(out.astype(np.float32))
```
